# Optimizing a Trainium2 kernel written in Bass

```python
import jax, jax.numpy as jnp
from jax import lax
import numpy as np

D_MODEL = 1024
BATCH = 16
SEQ = 2048
DEPTH = 1

CHUNK = 64
N_META = 16
PAD = CHUNK - N_META
A_KDIM = 128
A_HEADS = D_MODEL // A_KDIM
A_FDIM = A_HEADS * A_KDIM
A_WIDTH = D_MODEL // 2
A_VDIM = A_WIDTH // A_HEADS
POOL_WINDOWS = (2, 4, 8, 16)
B_GROUPS = 4
B_WIDTH = D_MODEL // 2
B_GDIM = B_WIDTH // B_GROUPS
N_BRANCH = 2
SPLITS = (A_FDIM, 2 * A_FDIM, 2 * A_FDIM + A_WIDTH, 2 * A_FDIM + 2 * A_WIDTH,
          2 * A_FDIM + 2 * A_WIDTH + B_WIDTH, 2 * A_FDIM + 2 * A_WIDTH + B_WIDTH + D_MODEL)
IN_COLS = 2 * A_FDIM + 2 * A_WIDTH + B_WIDTH + N_BRANCH * D_MODEL
D_FF = ((8 * D_MODEL // 3 + 127) // 128) * 128
CONV_W = 3
EPS = 1e-6

kernel_name = "hybrid_hgrn2_pool_convffn_block"


def rmsnorm(x, g):
    xf = x.astype(jnp.float32)
    y = xf * lax.rsqrt(jnp.mean(xf * xf, axis=-1, keepdims=True) + EPS)
    return (y * g.astype(jnp.float32)).astype(x.dtype)


def hgrn2_mixer(q, f_logit, i, og, lb, head_g):
    dt = i.dtype
    f32 = jnp.float32
    B_, T, _ = q.shape
    n_chunks = (T + PAD) // CHUNK
    lbf = lb.astype(f32)
    f = lbf + (1.0 - lbf) * jax.nn.sigmoid(f_logit.astype(f32))
    k = 1.0 - f
    logf = jnp.log(f)
    qf = jax.nn.silu(q.astype(f32))
    vf = i.astype(f32)

    def to_chunks(t, d):
        t = jnp.pad(t, ((0, 0), (PAD, 0), (0, 0)))
        return t.reshape(B_, n_chunks, CHUNK, A_HEADS, d).transpose(1, 0, 3, 2, 4)

    qc = to_chunks(qf, A_KDIM)
    kc = to_chunks(k, A_KDIM)
    vc = to_chunks(vf, A_VDIM)
    ac = jnp.cumsum(to_chunks(logf, A_KDIM), axis=3)
    causal = jnp.tril(jnp.ones((CHUNK, CHUNK), dtype=bool))[:, :, None]

    def step(S, inp):
        q_, k_, v_, a_ = inp
        o_inter = jnp.einsum('bhtk,bhkv->bhtv', q_ * jnp.exp(a_), S)
        diff = a_[:, :, :, None, :] - a_[:, :, None, :, :]
        decay = jnp.exp(jnp.where(causal, diff, -jnp.inf))
        scores = jnp.einsum('bhtsk,bhsk->bhts', q_[:, :, :, None, :] * decay, k_)
        o = o_inter + jnp.einsum('bhts,bhsv->bhtv', scores, v_)
        a_last = a_[:, :, -1:, :]
        S = jnp.exp(a_last[:, :, 0, :])[..., None] * S + jnp.einsum(
            'bhsk,bhsv->bhkv', k_ * jnp.exp(a_last - a_), v_)
        return S, o

    S0 = jnp.zeros((B_, A_HEADS, A_KDIM, A_VDIM), f32)
    _, o = lax.scan(step, S0, (qc, kc, vc, ac))
    o = o.transpose(1, 0, 3, 2, 4).reshape(B_, n_chunks * CHUNK, A_HEADS, A_VDIM)[:, PAD:]
    o = o * lax.rsqrt(jnp.mean(o * o, axis=-1, keepdims=True) + EPS)
    o = o * head_g.astype(f32).reshape(A_HEADS, A_VDIM)
    o = o.reshape(B_, T, A_WIDTH) * jax.nn.silu(og.astype(f32))
    return o.astype(dt)


def pool_mixer(p, w_group, scale):
    dt = p.dtype
    B_, T, _ = p.shape
    pf = p.astype(jnp.float32).reshape(B_, T, B_GROUPS, B_GDIM)
    cs = jnp.cumsum(pf, axis=1)
    pos = jnp.arange(1, T + 1, dtype=jnp.float32)
    outs = []
    for gi, w in enumerate(POOL_WINDOWS):
        c = cs[:, :, gi]
        lag = jnp.pad(c[:, :-w], ((0, 0), (w, 0), (0, 0)))
        mean = (c - lag) / jnp.minimum(pos, float(w))[None, :, None]
        outs.append(mean - pf[:, :, gi])
    d = jnp.stack(outs, axis=2)
    y = jnp.einsum('btgc,gcd->btgd', d, w_group.astype(jnp.float32)).reshape(B_, T, B_WIDTH)
    return (y * scale.astype(jnp.float32)).astype(dt)


def causal_dwconv(u, w, b):
    C = u.shape[-1]
    up = jnp.pad(u, ((0, 0), (CONV_W - 1, 0), (0, 0)))
    y = lax.conv_general_dilated(up, w[:, None, :].astype(u.dtype), window_strides=(1,), padding='VALID',
                                 dimension_numbers=('NWC', 'WIO', 'NWC'), feature_group_count=C)
    return y + b


def setup_inputs(seed: int = 0) -> dict:
    key = jax.random.key(seed)
    ks = jax.random.split(key, 20)
    f32 = jnp.float32
    nrm = lambda k, shape, s: jax.random.normal(k, shape, f32) * s
    L = DEPTH
    return {
        "x": nrm(ks[0], (BATCH, SEQ, D_MODEL), 1.0),
        "meta_tokens": nrm(ks[1], (N_META, D_MODEL), 1.0),
        "lb_logits": nrm(ks[2], (DEPTH + 1, A_FDIM), 1.0),
        "norm1_g": 1.0 + nrm(ks[3], (L, D_MODEL), 0.05),
        "w_in": nrm(ks[4], (L, D_MODEL, IN_COLS), D_MODEL ** -0.5),
        "b_f": nrm(ks[5], (L, A_FDIM), 0.1),
        "head_norm_g": 1.0 + nrm(ks[6], (L, A_WIDTH), 0.05),
        "w_pool": nrm(ks[7], (L, B_GROUPS, B_GDIM, B_GDIM), B_GDIM ** -0.5),
        "pool_scale": 1.0 + nrm(ks[8], (L, B_WIDTH), 0.1),
        "w_branch_a": nrm(ks[9], (L, A_WIDTH, D_MODEL), A_WIDTH ** -0.5),
        "w_branch_b": nrm(ks[10], (L, B_WIDTH, D_MODEL), B_WIDTH ** -0.5),
        "w_out": nrm(ks[11], (L, D_MODEL, D_MODEL), D_MODEL ** -0.5),
        "norm2_g": 1.0 + nrm(ks[12], (L, D_MODEL), 0.05),
        "w_up": nrm(ks[13], (L, D_MODEL, 2 * D_FF), D_MODEL ** -0.5),
        "conv_w": nrm(ks[14], (L, CONV_W, D_FF), CONV_W ** -0.5),
        "conv_b": nrm(ks[15], (L, D_FF), 0.01),
        "w_down": nrm(ks[16], (L, D_FF, D_MODEL), D_FF ** -0.5),
        "final_norm_g": 1.0 + nrm(ks[17], (D_MODEL,), 0.05),
    }


def reference(x, meta_tokens, lb_logits, norm1_g, w_in, b_f, head_norm_g, w_pool, pool_scale,
              w_branch_a, w_branch_b, w_out, norm2_g, w_up, conv_w, conv_b, w_down, final_norm_g):
    B_ = x.shape[0]
    meta = jnp.broadcast_to(meta_tokens.astype(x.dtype)[None], (B_, N_META, D_MODEL))
    h = jnp.concatenate([meta, x], axis=1)
    lb_all = jnp.cumsum(jax.nn.softmax(lb_logits.astype(jnp.float32), axis=0), axis=0)
    for l in range(DEPTH):
        z = rmsnorm(h, norm1_g[l])
        proj = z @ w_in[l]
        q, fl, iv, og, pin, ga, gb = jnp.split(proj, SPLITS, axis=-1)
        o_a = hgrn2_mixer(q, fl + b_f[l], iv, og, lb_all[l], head_norm_g[l])
        o_b = pool_mixer(pin, w_pool[l], pool_scale[l])
        mix = jax.nn.sigmoid(ga) * (o_a @ w_branch_a[l]) + jax.nn.sigmoid(gb) * (o_b @ w_branch_b[l])
        h = h + mix @ w_out[l]
        z = rmsnorm(h, norm2_g[l])
        u, v = jnp.split(z @ w_up[l], 2, axis=-1)
        u = causal_dwconv(u, conv_w[l], conv_b[l])
        h = h + (jax.nn.silu(u) * v) @ w_down[l]
    h = rmsnorm(h, final_norm_g)
    return h[:, N_META:]
```

```python
import numpy as np
import ml_dtypes
from contextlib import ExitStack
import concourse.bass as bass
import concourse.mybir as mybir
from concourse.bass_utils import run_bass_kernel_spmd

F32 = mybir.dt.float32
BF16 = mybir.dt.bfloat16
AF = mybir.ActivationFunctionType
ALU = mybir.AluOpType

D = 1024
NBS = 17
NBC = 2 * NBS
SEQP = NBS * 128
EPS = 1e-6
NS = 6
USE_SCRATCH = True


class Slot:
    __slots__ = ("w", "r")

    def __init__(self):
        self.w = None
        self.r = {}


class Buf:
    def __init__(self, name, ap):
        self.name = name
        self.ap = ap
        self.slots = [Slot()]
        self.dsem = None
        self.dcnt = 0

    def __getitem__(self, k):
        return self.ap[k]


class Eng:
    def __init__(self, name):
        self.name = name
        self.key = "E_" + name
        self.sem = None
        self.cnt = 0
        self.ops = []
        self.seen = {}


class K:
    def __init__(self, nc, es):
        self.nc = nc
        self.es = es
        self.sems = {}
        self.engs = {}
        for name in ("pe", "act", "dve", "pool", "sp"):
            e = Eng(name)
            self.engs[name] = e
            if name != "sp":
                e.sem = es.enter_context(nc.semaphore("s_" + name))
                self.sems[e.key] = e.sem
        self.psum_banks = []
        self.psum_rr = 0
        self.ninst = 0
        self.nwaits = 0

    def sb(self, name, shape, dt):
        t = self.es.enter_context(self.nc.sbuf_tensor("sb_" + name, list(shape), dt))
        return Buf(name, t[:])

    def pseudo(self, name):
        return Buf(name, None)

    def psum_init(self):
        for i in range(8):
            t = self.es.enter_context(self.nc.psum_tensor("psb%d" % i, [128, 512], F32))
            self.psum_banks.append(Buf("psb%d" % i, t[:]))

    def psum(self):
        b = self.psum_banks[self.psum_rr % 8]
        self.psum_rr += 1
        return b

    def _deps(self, eng, reads, writes):
        need = {}

        def add(tok):
            if tok is None:
                return
            k, v = tok
            if need.get(k, 0) < v:
                need[k] = v
        for b in reads:
            for s in b.slots:
                add(s.w)
        for b in writes:
            for s in b.slots:
                add(s.w)
                for k, v in s.r.items():
                    add((k, v))
        out = []
        for k, v in need.items():
            if eng.name == "pe" and k == eng.key:
                continue
            if eng.seen.get(k, 0) >= v:
                continue
            eng.seen[k] = v
            out.append((k, v))
        return out

    def _mark(self, tok, reads, writes):
        k, v = tok
        for b in writes:
            for s in b.slots:
                s.w = tok
                s.r = {}
        for b in reads:
            for s in b.slots:
                if s.r.get(k, 0) < v:
                    s.r[k] = v

    def op(self, engname, fn, reads=(), writes=(), signal=True):
        eng = self.engs[engname]
        deps = self._deps(eng, reads, writes)
        for k, v in deps:
            if k == eng.key:
                assert v <= eng.cnt, (engname, v, eng.cnt)
        if signal:
            eng.cnt += 1
            tok = (eng.key, eng.cnt)
        else:
            tok = (eng.key, eng.cnt + 1)
        self._mark(tok, reads, writes)
        self.nwaits += len(deps)
        self.ninst += 1
        eng.ops.append((deps, fn, signal))

    def dma(self, qname, fn, buf, reads=(), writes=()):
        eng = self.engs[qname]
        deps = self._deps(eng, reads, writes)
        if buf.dsem is None:
            buf.dsem = self.es.enter_context(self.nc.semaphore("d_" + buf.name))
            self.sems["D_" + buf.name] = buf.dsem
        buf.dcnt += 16
        tok = ("D_" + buf.name, buf.dcnt)
        self._mark(tok, reads, writes)
        self.nwaits += len(deps)
        self.ninst += 1
        eng.ops.append((deps, fn, ("dma", buf.dsem)))

    def final_wait(self, qname, bufs):
        eng = self.engs[qname]
        deps = self._deps(eng, (), bufs)
        eng.ops.append((deps, None, False))

    def run(self):
        nc = self.nc
        K_ = self

        def replay(name):
            def body(e):
                eng = K_.engs[name]
                for deps, fn, signal in eng.ops:
                    for k, v in deps:
                        e.wait_ge(K_.sems[k], v)
                    if fn is None:
                        continue
                    inst = fn(e)
                    if signal is True:
                        inst.then_inc(eng.sem, 1)
                    elif signal is False:
                        pass
                    else:
                        inst.then_inc(signal[1], 16)
            return body
        with nc.Block() as block:
            block.tensor(replay("pe"))
            block.scalar(replay("act"))
            block.vector(replay("dve"))
            block.gpsimd(replay("pool"))
            block.sync(replay("sp"))


C_G1, C_G2, C_BF, C_L0, C_L1 = 0, 8, 16, 24, 32
C_HG, C_PS = 40, 44
C_W0, C_W1, C_W2, C_CB = 48, 70, 92, 114
C_PAD = 136
NCOLS = 144


def tiles_of_core():
    t = []
    b = 0
    while b < NBC:
        nb = min(4, NBC - b)
        t.append((b, nb))
        b += nb
    return t


def weight_schedule():
    s = []
    for hp in range(4):
        s.append(("q%d" % hp, "w_in", hp * 256, None, 8, 256))
        s.append(("f%d" % hp, "w_in", 1024 + hp * 256, None, 8, 256))
    for i in range(2):
        s.append(("i%d" % i, "w_in", 2048 + i * 256, None, 8, 256))
    for i in range(2):
        s.append(("og%d" % i, "w_in", 2560 + i * 256, None, 8, 256))
    for i in range(2):
        s.append(("pl%d" % i, "w_in", 3072 + i * 256, None, 8, 256))
    for hf in range(2):
        s.append(("wa%d" % hf, "w_a", hf * 512, None, 4, 512))
        s.append(("wb%d" % hf, "w_b", hf * 512, None, 4, 512))
        for i in range(2):
            up = hf * 2 + i
            s.append(("ga%d" % up, "w_in", 3584 + up * 256, None, 8, 256))
            s.append(("gb%d" % up, "w_in", 4608 + up * 256, None, 8, 256))
    for q in range(4):
        s.append(("wo%d" % q, "w_out", q * 256, None, 8, 256))
    for mp in range(11):
        s.append(("uu%d" % mp, "w_up", mp * 256, None, 8, 256))
        s.append(("uv%d" % mp, "w_up", 2816 + mp * 256, None, 8, 256))
    for q in range(4):
        for mg in range(3):
            nm = 8 if mg < 2 else 6
            s.append(("wd%d_%d" % (q, mg), "w_down", q * 256, mg * 8, nm, 256))
    return s


def build_program(debug=False, ntiles=None):
    nc = bass.Bass("TRN2", target_bir_lowering=False)
    dr = {}
    dumps = {}

    def din(name, shape, dt=F32):
        dr[name] = nc.dram_tensor(name, list(shape), dt, kind="ExternalInput").ap()
        return dr[name]

    xin = din("xin", [NBC * 128, D])
    din("w_in", [D, 5632])
    din("w_up", [D, 5632])
    din("w_down", [2816, D])
    din("w_out", [D, D])
    din("w_a", [512, D])
    din("w_b", [512, D])
    w_pool = din("w_pool", [512, 128])
    cols_d = din("cols", [128, NCOLS])
    gF_d = din("gF", [128, D])
    ident_d = din("ident", [128, 128], BF16)
    cmask_d = din("cmask", [128, 512], mybir.dt.uint32)
    smask_d = din("smask", [128, 512])
    onesbd_d = din("onesbd", [128, 128], BF16)
    poolm_d = din("poolm", [128, 12 * 128], BF16)
    out = nc.dram_tensor("out", [2 * 2048, D], F32, kind="ExternalOutput").ap()
    sched = weight_schedule()
    NW = len(sched)
    scr = nc.dram_tensor("wscr", [NW, 128, 2048], BF16, kind="Internal").ap()

    wsrc = {
        "w_in": dr["w_in"].rearrange("(kc p) c -> p kc c", p=128),
        "w_up": dr["w_up"].rearrange("(kc p) c -> p kc c", p=128),
        "w_out": dr["w_out"].rearrange("(kc p) c -> p kc c", p=128),
        "w_a": dr["w_a"].rearrange("(kc p) c -> p kc c", p=128),
        "w_b": dr["w_b"].rearrange("(kc p) c -> p kc c", p=128),
        "w_down": dr["w_down"].rearrange("(m p) c -> p m c", p=128),
    }

    with ExitStack() as es:
        k = K(nc, es)
        k.psum_init()
        cols = k.sb("cols", [128, NCOLS], F32)
        gF = k.sb("gF", [128, D], F32)
        ident = k.sb("ident", [128, 128], BF16)
        cmask = k.sb("cmask", [128, 512], mybir.dt.uint32)
        smask = k.sb("smask", [128, 512], F32)
        onesbd = k.sb("onesbd", [128, 128], BF16)
        poolm = k.sb("poolm", [128, 12, 128], BF16)
        wpool = k.sb("wpool", [128, 4, 128], BF16)
        lbt = k.sb("lbt", [128, 8], F32)
        omlt = k.sb("omlt", [128, 8], F32)
        nomlt = k.sb("nomlt", [128, 8], F32)
        ldiff = k.sb("ldiff", [128, 8], F32)
        for b_, d_ in ((cols, cols_d), (gF, gF_d), (ident, ident_d), (cmask, cmask_d), (smask, smask_d), (onesbd, onesbd_d)):
            k.dma("sp", lambda e, b_=b_, d_=d_: e.dma_start(out=b_[:], in_=d_), b_, writes=[b_])
        k.dma("sp", lambda e: e.dma_start(out=poolm[:], in_=poolm_d.rearrange("p (m t) -> p m t", t=128)), poolm, writes=[poolm])
        k.dma("pool", lambda e: e.dma_start(out=wpool[:], in_=w_pool.rearrange("(g c) d -> c g d", c=128)), wpool, writes=[wpool])
        k.op("dve", lambda e: e.tensor_tensor(out=ldiff[:], in0=cols[:, C_L0:C_L0 + 8], in1=cols[:, C_L1:C_L1 + 8], op=ALU.subtract), reads=[cols], writes=[ldiff])
        k.op("act", lambda e: e.activation(out=lbt[:], in_=ldiff[:], func=AF.Sigmoid), reads=[ldiff], writes=[lbt])
        k.op("act", lambda e: e.activation(out=omlt[:], in_=ldiff[:], func=AF.Sigmoid, scale=-1.0), reads=[ldiff], writes=[omlt])
        k.op("dve", lambda e: e.tensor_scalar(out=nomlt[:], in0=omlt[:], scalar1=-1.0, scalar2=None, op0=ALU.mult), reads=[omlt], writes=[nomlt])

        S = k.sb("S", [128, 8, 64], F32)
        Sb = [k.sb("Sb%d" % i, [128, 8, 64], BF16) for i in range(2)]
        Stmp = k.sb("Stmp", [128, 8, 64], F32)
        halo = k.sb("halo", [128, 22, 2], F32)
        Mt = k.sb("Mt", [128, 8, 8], F32)
        dSt = k.sb("dSt", [128, 8, 8], F32)
        Et = k.sb("Et", [128, 8, 8], F32)
        ptok = [k.sb("ptok%d" % i, [128, 4, 512], BF16) for i in range(2)]
        k.op("pool", lambda e: e.memset(halo[:], 0.0), writes=[halo])

        h = k.sb("h", [128, 4, D], F32)
        junk = k.sb("junk", [128, D], BF16)
        ss = k.sb("ss", [128, 4], F32)
        rs = k.sb("rs", [128, 4], F32)
        ztok = k.sb("ztok", [128, 4, D], BF16)
        zf = [k.sb("zf%d" % i, [128, 512], BF16) for i in range(8)]
        qt = [k.sb("qt%d" % i, [128, 512], BF16) for i in range(8)]
        kt = [k.sb("kt%d" % i, [128, 512], BF16) for i in range(8)]
        kT = [k.sb("kT%d" % i, [128, 1024], BF16) for i in range(4)]
        z2f = zf
        gg = qt + kt + [k.sb("gg%d" % i, [128, 512], BF16) for i in range(16, 22)]
        vtok = k.sb("vtok", [128, 4, 512], BF16)
        sog = k.sb("sog", [128, 4, 512], BF16)
        oa = k.sb("oa", [128, 4, 512], BF16)
        ob = k.sb("ob", [128, 4, 512], BF16)
        dfeat = k.sb("dfeat", [128, 4, 512], BF16)
        mix = [k.sb("mix%d" % i, [128, 512], BF16) for i in range(8)]
        qs = k.sb("qs", [128, 512], F32)
        sg = k.sb("sg", [128, 512], F32)
        lf = k.sb("lf", [128, 512], F32)
        kk = k.sb("kk", [128, 512], F32)
        aa = k.sb("aa", [128, 512], F32)
        ar = k.sb("ar", [128, 512], F32)
        ep = k.sb("ep", [128, 512], F32)
        en = k.sb("en", [128, 512], F32)
        msk = [k.sb("msk%d" % i, [128, 512], BF16) for i in range(2)]
        for m_ in msk:
            k.op("pool", lambda e, m_=m_: e.memset(m_[:], 0.0), writes=[m_])
        osq = k.sb("osq", [128, 512], BF16)
        ors = k.sb("ors", [128, 512], F32)
        on = k.sb("on", [128, 512], F32)
        on2 = on
        sga = k.sb("sga", [128, 512], BF16)
        sgb = k.sb("sgb", [128, 512], BF16)
        t1 = k.sb("t1", [128, 512], F32)
        t2 = k.sb("t2", [128, 512], F32)
        ub = [k.sb("ub%d" % i, [128, 516], F32) for i in range(2)]
        c0 = [k.sb("c0_%d" % i, [128, 512], F32) for i in range(2)]
        c2 = [k.sb("c2_%d" % i, [128, 512], F32) for i in range(2)]
        c1 = [k.sb("c1_%d" % i, [128, 512], F32) for i in range(2)]
        ot = k.sb("ot", [128, D], F32)
        ring = [k.sb("wr%d" % i, [128, 2048], BF16) for i in range(NS)]
        scrb = [k.pseudo("scr%d" % i) for i in range(NW)]

        tiles = tiles_of_core()
        if ntiles is not None:
            tiles = tiles[:ntiles]

        def dump(name, buf, ap=None):
            if not debug or name in dumps:
                return
            src = buf.ap if ap is None else ap
            dt_ = src.dtype
            d = nc.dram_tensor('dbg_' + name, list(src.shape), dt_, kind='ExternalOutput').ap()
            dumps[name] = d
            k.dma('sp', lambda e, d=d, src=src: e.dma_start(out=d, in_=src), buf, reads=[buf])
        glob = [(ti, wi) for ti in range(len(tiles)) for wi in range(NW)]
        wstate = dict(emitted=0, next=0)
        loaded = {}

        def emit_load(gi):
            ti, wi = glob[gi]
            key, which, a, b, nk, ncol = sched[wi]
            slot = ring[gi % NS]
            view = slot.ap[:, 0:nk * ncol].rearrange("p (k c) -> p k c", c=ncol)
            if ti == 0 or not USE_SCRATCH:
                if which == "w_down":
                    src = wsrc[which][:, b:b + nk, a:a + ncol]
                else:
                    src = wsrc[which][:, :, a:a + ncol]
                k.dma("pool", lambda e, view=view, src=src: e.dma_start(out=view, in_=src), slot, writes=[slot])
                if USE_SCRATCH:
                    k.dma("sp", lambda e, slot=slot, wi=wi: e.dma_start(out=scr[wi], in_=slot[:]), slot, reads=[slot], writes=[scrb[wi]])
            else:
                k.dma("sp", lambda e, slot=slot, wi=wi: e.dma_start(out=slot[:], in_=scr[wi]), slot, reads=[scrb[wi]], writes=[slot])
            loaded[gi] = (slot, view)

        released = set()
        cur_gi = {}

        def pump():
            while wstate["emitted"] < len(glob):
                g_ = wstate["emitted"]
                if g_ - NS >= 0 and (g_ - NS) not in released:
                    break
                emit_load(g_)
                wstate["emitted"] += 1

        def wget(key):
            gi = wstate["next"]
            wstate["next"] += 1
            assert sched[glob[gi][1]][0] == key, (sched[glob[gi][1]][0], key)
            pump()
            assert gi in loaded, ("weight ring too small", key)
            slot, view = loaded.pop(gi)
            cur_gi[slot.name] = gi
            return slot, view

        def wrel(*slots):
            for sl in slots:
                released.add(cur_gi.pop(sl.name))
            pump()

        def mm(out_, lhsT, rhs, start, stop):
            return lambda e: e.matmul(out_, lhsT=lhsT, rhs=rhs, start=start, stop=stop)

        def norm_to_feat(nb, TT, gcol, zdst):
            for b in range(nb):
                k.op("act", lambda e, b=b: e.activation(out=junk[:], in_=h[:, b, :], func=AF.Square, accum_out=ss[:, b:b + 1]),
                     reads=[h], writes=[junk, ss])
            k.op("act", lambda e: e.activation(out=rs[:, 0:nb], in_=ss[:, 0:nb], func=AF.Sqrt, scale=1.0 / D, bias=EPS), reads=[ss], writes=[rs])
            k.op("dve", lambda e: e.reciprocal(out=rs[:, 0:nb], in_=rs[:, 0:nb]), reads=[rs], writes=[rs])
            for b in range(nb):
                k.op("dve", lambda e, b=b: e.tensor_scalar(out=ztok[:, b, :], in0=h[:, b, :], scalar1=rs[:, b:b + 1], scalar2=None, op0=ALU.mult),
                     reads=[h, rs], writes=[ztok])
            for kc in range(8):
                pb = k.psum()
                pv = pb.ap.bitcast(BF16)
                for b in range(nb):
                    k.op("pe", lambda e, b=b, kc=kc, pv=pv: e.transpose(out=pv[:, b * 128:(b + 1) * 128], in_=ztok[:, b, kc * 128:(kc + 1) * 128], identity=ident[:]),
                         reads=[ztok, ident], writes=[pb], signal=(b == nb - 1))
                if kc % 2 == 0:
                    k.op("act", lambda e, kc=kc, pv=pv: e.activation(out=zdst[kc][:, 0:TT], in_=pv[:, 0:TT], func=AF.Identity, scale=cols[:, gcol + kc:gcol + kc + 1]),
                         reads=[pb, cols], writes=[zdst[kc]])
                else:
                    k.op("dve", lambda e, kc=kc, pv=pv: e.tensor_scalar(out=zdst[kc][:, 0:TT], in0=pv[:, 0:TT], scalar1=cols[:, gcol + kc:gcol + kc + 1], scalar2=None, op0=ALU.mult),
                         reads=[pb, cols], writes=[zdst[kc]])

        def feat_mm(pb, wview, col0, zsrc, TT):
            nk = 8
            for kc in range(nk):
                k.op("pe", mm(pb[:, 0:TT], wview[:, kc, col0:col0 + 128], zsrc[kc][:, 0:TT], kc == 0, kc == nk - 1),
                     reads=[zsrc[kc], wview_buf[0]], writes=[pb], signal=(kc == nk - 1))

        wview_buf = [None]

        def do_tile(ti, gb0, nb):
            TT = nb * 128
            ncn = nb * 2
            pt_cur = ptok[ti % 2]
            pt_prev = ptok[(ti + 1) % 2]
            k.dma("sp", lambda e, gb0=gb0, nb=nb: e.dma_start(out=h[:, 0:nb, :], in_=xin.rearrange("(b p) d -> p b d", p=128)[:, gb0:gb0 + nb, :]), h, writes=[h])
            norm_to_feat(nb, TT, C_G1, zf)
            for i_ in range(8):
                dump('zf%d' % i_, zf[i_])
            for hp in range(4):
                qslot, qview = wget("q%d" % hp)
                fslot, fview = wget("f%d" % hp)
                for jj in range(2):
                    j = hp * 2 + jj
                    pq = k.psum()
                    wview_buf[0] = qslot
                    feat_mm(pq, qview, jj * 128, zf, TT)
                    pf = k.psum()
                    wview_buf[0] = fslot
                    feat_mm(pf, fview, jj * 128, zf, TT)
                    k.op("act", lambda e, pq=pq: e.activation(out=qs[:, 0:TT], in_=pq[:, 0:TT], func=AF.Silu), reads=[pq], writes=[qs])
                    k.op("act", lambda e, pf=pf, j=j: e.activation(out=sg[:, 0:TT], in_=pf[:, 0:TT], func=AF.Sigmoid, bias=cols[:, C_BF + j:C_BF + j + 1]),
                         reads=[pf, cols], writes=[sg])
                    k.op("act", lambda e, j=j: e.activation(out=lf[:, 0:TT], in_=sg[:, 0:TT], func=AF.Ln, scale=omlt[:, j:j + 1], bias=lbt[:, j:j + 1]),
                         reads=[sg, omlt, lbt], writes=[lf])
                    k.op("pool", lambda e, j=j: e.tensor_scalar(out=kk[:, 0:TT], in0=sg[:, 0:TT], scalar1=nomlt[:, j:j + 1], scalar2=omlt[:, j:j + 1], op0=ALU.mult, op1=ALU.add),
                         reads=[sg, nomlt, omlt], writes=[kk])
                    k.op("dve", lambda e: e.tensor_tensor_scan(out=aa[:, 0:TT], data0=smask[:, 0:TT], data1=lf[:, 0:TT], initial=0.0, op0=ALU.mult, op1=ALU.add),
                         reads=[smask, lf], writes=[aa])
                    a3 = aa.ap.rearrange("p (c t) -> p c t", t=64)
                    ar3 = ar.ap.rearrange("p (c t) -> p c t", t=64)
                    ep3 = ep.ap.rearrange("p (c t) -> p c t", t=64)
                    k.op("dve", lambda e, a3=a3, ar3=ar3: e.tensor_tensor(out=ar3[:, 0:ncn, :], in0=a3[:, 0:ncn, :], in1=a3[:, 0:ncn, 31:32].to_broadcast([128, ncn, 64]), op=ALU.subtract),
                         reads=[aa], writes=[ar])
                    k.op("act", lambda e, a3=a3, j=j: e.activation(out=Mt[:, j, 0:ncn], in_=a3[:, 0:ncn, 31], func=AF.Exp), reads=[aa], writes=[Mt])
                    k.op("act", lambda e, a3=a3, j=j: e.activation(out=dSt[:, j, 0:ncn], in_=a3[:, 0:ncn, 63], func=AF.Exp), reads=[aa], writes=[dSt])
                    k.op("act", lambda e: e.activation(out=ep[:, 0:TT], in_=ar[:, 0:TT], func=AF.Exp), reads=[ar], writes=[ep])
                    k.op("act", lambda e: e.activation(out=en[:, 0:TT], in_=ar[:, 0:TT], func=AF.Exp, scale=-1.0), reads=[ar], writes=[en])
                    k.op("pool", lambda e, ep3=ep3, j=j: e.tensor_copy(out=Et[:, j, 0:ncn], in_=ep3[:, 0:ncn, 63]), reads=[ep], writes=[Et])
                    k.op("dve", lambda e, j=j: e.tensor_tensor(out=qt[j][:, 0:TT], in0=qs[:, 0:TT], in1=ep[:, 0:TT], op=ALU.mult), reads=[qs, ep], writes=[qt[j]])
                    k.op("pool", lambda e, j=j: e.tensor_tensor(out=kt[j][:, 0:TT], in0=kk[:, 0:TT], in1=en[:, 0:TT], op=ALU.mult), reads=[kk, en], writes=[kt[j]])
                wrel(qslot, fslot)
            for i_ in range(8):
                dump('qt%d' % i_, qt[i_])
                dump('kt%d' % i_, kt[i_])
            dump('Mt', Mt)
            dump('dSt', dSt)
            dump('Et', Et)
            for i in range(2):
                slot, view = wget("i%d" % i)
                for b in range(nb):
                    pb = k.psum()
                    for kc in range(8):
                        k.op("pe", mm(pb[:, 0:256], zf[kc][:, b * 128:(b + 1) * 128], view[:, kc, :], kc == 0, kc == 7),
                             reads=[zf[kc], slot], writes=[pb], signal=(kc == 7))
                    k.op("act", lambda e, pb=pb, b=b, i=i: e.activation(out=vtok[:, b, i * 256:(i + 1) * 256], in_=pb[:, 0:256], func=AF.Copy), reads=[pb], writes=[vtok])
                wrel(slot)
            for i in range(2):
                slot, view = wget("og%d" % i)
                wview_buf[0] = slot
                for jj in range(2):
                    u = i * 2 + jj
                    pb = k.psum()
                    feat_mm(pb, view, jj * 128, zf, TT)
                    k.op("act", lambda e, pb=pb, u=u: e.activation(out=sog[:, u, 0:TT], in_=pb[:, 0:TT], func=AF.Silu), reads=[pb], writes=[sog])
                wrel(slot)
            for i in range(2):
                slot, view = wget("pl%d" % i)
                for b in range(nb):
                    pb = k.psum()
                    for kc in range(8):
                        k.op("pe", mm(pb[:, 0:256], zf[kc][:, b * 128:(b + 1) * 128], view[:, kc, :], kc == 0, kc == 7),
                             reads=[zf[kc], slot], writes=[pb], signal=(kc == 7))
                    k.op("dve", lambda e, pb=pb, b=b, i=i: e.tensor_copy(out=pt_cur[:, b, i * 256:(i + 1) * 256], in_=pb[:, 0:256]), reads=[pb], writes=[pt_cur])
                wrel(slot)
            dump('vtok', vtok)
            dump('sog', sog)
            dump('ptok', pt_cur)
            for b in range(nb):
                gb = gb0 + b
                ca, cb = 2 * b, 2 * b + 1
                if gb % NBS == 0:
                    k.op("dve", lambda e: e.memset(S[:], 0.0), writes=[S])
                k.op("dve", lambda e, ca=ca: e.tensor_tensor(out=Sb[0][:], in0=S[:], in1=Mt[:, :, ca:ca + 1].to_broadcast([128, 8, 64]), op=ALU.mult),
                     reads=[S, Mt], writes=[Sb[0]])
                pb = k.psum()
                pv = pb.ap.bitcast(BF16)
                for j in range(8):
                    k.op("pe", lambda e, j=j, b=b, pv=pv: e.transpose(out=pv[:, j * 128:(j + 1) * 128], in_=kt[j][:, b * 128:(b + 1) * 128], identity=ident[:]),
                         reads=[kt[j], ident], writes=[pb], signal=(j == 7))
                k.op("act", lambda e, pv=pv, b=b: e.activation(out=kT[b][:], in_=pv, func=AF.Copy), reads=[pb], writes=[kT[b]])
                P = []
                for cc in range(2):
                    pp = k.psum()
                    P.append(pp)
                    r0 = cc * 64
                    for j in range(8):
                        k.op("pe", mm(pp[:, j * 64:(j + 1) * 64], kT[b][r0:r0 + 64, j * 128:(j + 1) * 128], vtok[r0:r0 + 64, b, j * 64:(j + 1) * 64], True, True),
                             reads=[kT[b], vtok], writes=[pp], signal=(j == 7))
                for half in range(2):
                    pb = k.psum()
                    for jj in range(4):
                        j = half * 4 + jj
                        k.op("pe", mm(pb[:, jj * 128:(jj + 1) * 128], kt[j][:, b * 128:(b + 1) * 128], qt[j][:, b * 128:(b + 1) * 128], True, True),
                             reads=[kt[j], qt[j]], writes=[pb], signal=(jj == 3))
                    k.op("dve", lambda e, pb=pb, half=half: e.copy_predicated(out=msk[half][:], mask=cmask[:], data=pb[:]),
                         reads=[pb, cmask, msk[half]], writes=[msk[half]])
                S3 = S.ap
                for cc, cidx in ((0, ca), (1, cb)):
                    pp = P[cc]
                    k.op("dve", lambda e, cidx=cidx: e.tensor_tensor(out=S[:], in0=S[:], in1=dSt[:, :, cidx:cidx + 1].to_broadcast([128, 8, 64]), op=ALU.mult),
                         reads=[S, dSt], writes=[S])
                    k.op("dve", lambda e, cidx=cidx, pp=pp: e.tensor_tensor(out=Stmp[:], in0=pp.ap.rearrange("p (j v) -> p j v", v=64),
                                                                          in1=Et[:, :, cidx:cidx + 1].to_broadcast([128, 8, 64]), op=ALU.mult),
                         reads=[pp, Et], writes=[Stmp])
                    k.op("dve", lambda e: e.tensor_tensor(out=S[:], in0=S[:], in1=Stmp[:], op=ALU.add), reads=[S, Stmp], writes=[S])
                    if cc == 0:
                        k.op("dve", lambda e, cb=cb: e.tensor_tensor(out=Sb[1][:], in0=S[:], in1=Mt[:, :, cb:cb + 1].to_broadcast([128, 8, 64]), op=ALU.mult),
                             reads=[S, Mt], writes=[Sb[1]])
                po = k.psum()
                for j in range(8):
                    r0 = (j % 2) * 64
                    cbase = (j // 2) * 128
                    half, jj = j // 4, j % 4
                    k.op("pe", mm(po[r0:r0 + 64, cbase:cbase + 128], vtok[:, b, j * 64:(j + 1) * 64], msk[half][:, jj * 128:(jj + 1) * 128], True, False),
                         reads=[vtok, msk[half]], writes=[po], signal=False)
                    k.op("pe", mm(po[r0:r0 + 64, cbase:cbase + 64], Sb[0][:, j, :], qt[j][:, b * 128:b * 128 + 64], False, False),
                         reads=[Sb[0], qt[j]], writes=[po], signal=False)
                    k.op("pe", mm(po[r0:r0 + 64, cbase + 64:cbase + 128], Sb[1][:, j, :], qt[j][:, b * 128 + 64:b * 128 + 128], False, True),
                         reads=[Sb[1], qt[j]], writes=[po], signal=(j == 7))
                k.op("act", lambda e, po=po: e.activation(out=osq[:], in_=po[:], func=AF.Square), reads=[po], writes=[osq])
                pss = k.psum()
                k.op("pe", mm(pss[:], onesbd[:], osq[:], True, True), reads=[onesbd, osq], writes=[pss])
                k.op("act", lambda e, pss=pss: e.activation(out=ors[:], in_=pss[:], func=AF.Sqrt, scale=1.0 / 64, bias=EPS), reads=[pss], writes=[ors])
                k.op("dve", lambda e: e.reciprocal(out=ors[:], in_=ors[:]), reads=[ors], writes=[ors])
                k.op("dve", lambda e, po=po: e.tensor_tensor(out=on[:], in0=po[:], in1=ors[:], op=ALU.mult), reads=[po, ors], writes=[on])
                k.op("pool", lambda e, b=b: e.tensor_tensor(out=on2.ap.rearrange("p (j t) -> p j t", t=128), in0=on.ap.rearrange("p (j t) -> p j t", t=128),
                                                          in1=sog[:, :, b * 128:(b + 1) * 128], op=ALU.mult), reads=[on, sog], writes=[on2])
                k.op("pool", lambda e, b=b: e.tensor_tensor(out=oa[:, :, b * 128:(b + 1) * 128], in0=on2.ap.rearrange("p (j t) -> p j t", t=128),
                                                          in1=cols[:, C_HG:C_HG + 4].unsqueeze(2).to_broadcast([128, 4, 128]), op=ALU.mult),
                     reads=[on2, cols], writes=[oa])
            dump('oa', oa)
            dump('S', S)
            for b in range(nb):
                gb = gb0 + b
                pb = k.psum()
                for g in range(4):
                    cur = pt_cur[:, b, g * 128:(g + 1) * 128]
                    if gb % NBS == 0:
                        k.op("pe", mm(pb[:, g * 128:(g + 1) * 128], cur, poolm[:, g * 3 + 2, :], True, True), reads=[pt_cur, poolm], writes=[pb], signal=(g == 3))
                    else:
                        if b == 0:
                            prevb, prev = pt_prev, pt_prev[:, 3, g * 128:(g + 1) * 128]
                        else:
                            prevb, prev = pt_cur, pt_cur[:, b - 1, g * 128:(g + 1) * 128]
                        k.op("pe", mm(pb[:, g * 128:(g + 1) * 128], cur, poolm[:, g * 3 + 0, :], True, False), reads=[pt_cur, poolm], writes=[pb], signal=False)
                        k.op("pe", mm(pb[:, g * 128:(g + 1) * 128], prev, poolm[:, g * 3 + 1, :], False, True), reads=[prevb, poolm], writes=[pb], signal=(g == 3))
                k.op("act", lambda e, pb=pb, b=b: e.activation(out=dfeat[:, :, b * 128:(b + 1) * 128], in_=pb.ap.rearrange("p (g t) -> p g t", t=128), func=AF.Copy),
                     reads=[pb], writes=[dfeat])
            for g in range(4):
                pb = k.psum()
                k.op("pe", mm(pb[:, 0:TT], wpool[:, g, :], dfeat[:, g, 0:TT], True, True), reads=[wpool, dfeat], writes=[pb])
                k.op("dve", lambda e, pb=pb, g=g: e.tensor_scalar(out=ob[:, g, 0:TT], in0=pb[:, 0:TT], scalar1=cols[:, C_PS + g:C_PS + g + 1], scalar2=None, op0=ALU.mult),
                     reads=[pb, cols], writes=[ob])
            dump('ob', ob)
            for hf in range(2):
                waslot, waview = wget("wa%d" % hf)
                wbslot, wbview = wget("wb%d" % hf)
                for i in range(2):
                    up = hf * 2 + i
                    gaslot, gaview = wget("ga%d" % up)
                    gbslot, gbview = wget("gb%d" % up)
                    for jj in range(2):
                        u = up * 2 + jj
                        for (gslot, gview, wslot_, wview_, src, sgt, tt) in ((gaslot, gaview, waslot, waview, oa, sga, t1), (gbslot, gbview, wbslot, wbview, ob, sgb, t2)):
                            pg = k.psum()
                            wview_buf[0] = gslot
                            feat_mm(pg, gview, jj * 128, zf, TT)
                            k.op("act", lambda e, pg=pg, sgt=sgt: e.activation(out=sgt[:, 0:TT], in_=pg[:, 0:TT], func=AF.Sigmoid), reads=[pg], writes=[sgt])
                            pa = k.psum()
                            cu = (u % 4) * 128
                            for kc in range(4):
                                k.op("pe", mm(pa[:, 0:TT], wview_[:, kc, cu:cu + 128], src[:, kc, 0:TT], kc == 0, kc == 3),
                                     reads=[wslot_, src], writes=[pa], signal=(kc == 3))
                            k.op("dve", lambda e, pa=pa, sgt=sgt, tt=tt: e.tensor_tensor(out=tt[:, 0:TT], in0=pa[:, 0:TT], in1=sgt[:, 0:TT], op=ALU.mult),
                                 reads=[pa, sgt], writes=[tt])
                        k.op("pool", lambda e, u=u: e.tensor_tensor(out=mix[u][:, 0:TT], in0=t1[:, 0:TT], in1=t2[:, 0:TT], op=ALU.add), reads=[t1, t2], writes=[mix[u]])
                    wrel(gaslot, gbslot)
                wrel(waslot, wbslot)
            for i_ in range(8):
                dump('mix%d' % i_, mix[i_])
            for q in range(4):
                slot, view = wget("wo%d" % q)
                for b in range(nb):
                    gb = gb0 + b
                    pb = k.psum()
                    for u in range(8):
                        k.op("pe", mm(pb[:, 0:256], mix[u][:, b * 128:(b + 1) * 128], view[:, u, :], u == 0, u == 7),
                             reads=[mix[u], slot], writes=[pb], signal=(u == 7))
                    if gb % NBS == NBS - 1:
                        k.op("dve", lambda e, pb=pb, b=b, q=q: e.scalar_tensor_tensor(out=h[:, b, q * 256:(q + 1) * 256], in0=pb[:, 0:256], scalar=cols[:, C_PAD:C_PAD + 1],
                                                                                       in1=h[:, b, q * 256:(q + 1) * 256], op0=ALU.mult, op1=ALU.add),
                             reads=[pb, h, cols], writes=[h])
                    else:
                        k.op("dve", lambda e, pb=pb, b=b, q=q: e.tensor_tensor(out=h[:, b, q * 256:(q + 1) * 256], in0=pb[:, 0:256], in1=h[:, b, q * 256:(q + 1) * 256], op=ALU.add),
                             reads=[pb, h], writes=[h])
                wrel(slot)
            dump('h1', h)
            norm_to_feat(nb, TT, C_G2, z2f)
            for mp in range(11):
                uslot, uview = wget("uu%d" % mp)
                vslot, vview = wget("uv%d" % mp)
                for jj in range(2):
                    m = mp * 2 + jj
                    par = m % 2
                    pu = k.psum()
                    wview_buf[0] = uslot
                    feat_mm(pu, uview, jj * 128, z2f, TT)
                    pvv = k.psum()
                    wview_buf[0] = vslot
                    feat_mm(pvv, vview, jj * 128, z2f, TT)
                    ubm, c0m, c2m = ub[par], c0[par], c2[par]
                    k.op("pool", lambda e, ubm=ubm, m=m: e.tensor_copy(out=ubm[:, 0:2], in_=halo[:, m, :]), reads=[halo], writes=[ubm])
                    k.op("act", lambda e, ubm=ubm, pu=pu: e.activation(out=ubm[:, 2:2 + TT], in_=pu[:, 0:TT], func=AF.Copy), reads=[pu], writes=[ubm])
                    k.op("pool", lambda e, ubm=ubm, m=m: e.tensor_copy(out=halo[:, m, :], in_=ubm[:, TT:TT + 2]), reads=[ubm], writes=[halo])
                    c1m = c1[par]
                    k.op("act", lambda e, ubm=ubm, c1m=c1m, m=m: e.activation(out=c1m[:, 0:TT], in_=ubm[:, 1:1 + TT], func=AF.Identity, scale=cols[:, C_W1 + m:C_W1 + m + 1],
                                                                             bias=cols[:, C_CB + m:C_CB + m + 1]),
                         reads=[ubm, cols], writes=[c1m])
                    k.op("pool", lambda e, ubm=ubm, c0m=c0m, m=m: e.tensor_scalar(out=c0m[:, 0:TT], in0=ubm[:, 0:TT], scalar1=cols[:, C_W0 + m:C_W0 + m + 1],
                                                                               scalar2=0.0, op0=ALU.mult, op1=ALU.add),
                         reads=[ubm, cols], writes=[c0m])
                    k.op("pool", lambda e, c0m=c0m, c1m=c1m: e.tensor_tensor(out=c0m[:, 0:TT], in0=c0m[:, 0:TT], in1=c1m[:, 0:TT], op=ALU.add),
                         reads=[c0m, c1m], writes=[c0m])
                    k.op("dve", lambda e, pu=pu, c0m=c0m, c2m=c2m, m=m: e.scalar_tensor_tensor(out=c2m[:, 0:TT], in0=pu[:, 0:TT], scalar=cols[:, C_W2 + m:C_W2 + m + 1],
                                                                                            in1=c0m[:, 0:TT], op0=ALU.mult, op1=ALU.add),
                         reads=[pu, c0m, cols], writes=[c2m])
                    k.op("act", lambda e, c2m=c2m: e.activation(out=c2m[:, 0:TT], in_=c2m[:, 0:TT], func=AF.Silu), reads=[c2m], writes=[c2m])
                    k.op("dve", lambda e, c2m=c2m, pvv=pvv, m=m: e.tensor_tensor(out=gg[m][:, 0:TT], in0=pvv[:, 0:TT], in1=c2m[:, 0:TT], op=ALU.mult),
                         reads=[pvv, c2m], writes=[gg[m]])
                wrel(uslot, vslot)
            for i_ in range(22):
                dump('gg%d' % i_, gg[i_])
            for q in range(4):
                ws = [wget("wd%d_%d" % (q, mg)) for mg in range(3)]
                for b in range(nb):
                    pb = k.psum()
                    for m in range(22):
                        slot, view = ws[m // 8]
                        k.op("pe", mm(pb[:, 0:256], gg[m][:, b * 128:(b + 1) * 128], view[:, m % 8, :], m == 0, m == 21),
                             reads=[gg[m], slot], writes=[pb], signal=(m == 21))
                    k.op("dve", lambda e, pb=pb, b=b, q=q: e.tensor_tensor(out=h[:, b, q * 256:(q + 1) * 256], in0=pb[:, 0:256], in1=h[:, b, q * 256:(q + 1) * 256], op=ALU.add),
                         reads=[pb, h], writes=[h])
                wrel(*[w_[0] for w_ in ws])
            dump('h2', h)
            for b in range(nb):
                k.op("act", lambda e, b=b: e.activation(out=junk[:], in_=h[:, b, :], func=AF.Square, accum_out=ss[:, b:b + 1]), reads=[h], writes=[junk, ss])
            k.op("act", lambda e: e.activation(out=rs[:, 0:nb], in_=ss[:, 0:nb], func=AF.Sqrt, scale=1.0 / D, bias=EPS), reads=[ss], writes=[rs])
            k.op("dve", lambda e: e.reciprocal(out=rs[:, 0:nb], in_=rs[:, 0:nb]), reads=[rs], writes=[rs])
            for b in range(nb):
                gb = gb0 + b
                s_, j_ = gb // NBS, gb % NBS
                k.op("dve", lambda e, b=b: e.scalar_tensor_tensor(out=ot[:], in0=h[:, b, :], scalar=rs[:, b:b + 1], in1=gF[:], op0=ALU.mult, op1=ALU.mult),
                     reads=[h, rs, gF], writes=[ot])
                if j_ == 0:
                    r0, r1, o0 = 16, 128, s_ * 2048
                elif j_ == NBS - 1:
                    r0, r1, o0 = 0, 16, s_ * 2048 + 2032
                else:
                    r0, r1, o0 = 0, 128, s_ * 2048 + j_ * 128 - 16
                k.dma("sp", lambda e, r0=r0, r1=r1, o0=o0: e.dma_start(out=out[o0:o0 + (r1 - r0), :], in_=ot[r0:r1, :]), ot, reads=[ot])
        for ti_, (gb0_, nb_) in enumerate(tiles):
            do_tile(ti_, gb0_, nb_)
        k.final_wait("sp", [ot])
        if debug:
            k.final_wait('sp', [zf[0], qt[0], kt[0], Mt, dSt, Et, vtok, sog, ptok[0], oa, S, ob, mix[0], h] + zf + qt + kt + mix + gg)
        k.run()
        build_program.stats = (k.ninst, k.nwaits, {n: len(e.ops) for n, e in k.engs.items()})
    return nc


def _consts():
    bf = ml_dtypes.bfloat16
    ident = np.eye(128, dtype=np.float32).astype(bf)
    s = np.arange(128)[:, None]
    t = np.arange(128)[None, :]
    cmask = np.ascontiguousarray(np.tile(((s // 64 == t // 64) & (s <= t)).astype(np.uint32), (1, 4)))
    smask = np.ones((128, 512), np.float32)
    smask[:, ::64] = 0.0
    onesbd = (s // 64 == t // 64).astype(np.float32).astype(bf)
    pm = np.zeros((12, 128, 128), np.float32)
    for g, w in enumerate((2, 4, 8, 16)):
        cur = ((s <= t) & (s > t - w)).astype(np.float32) / w - (s == t).astype(np.float32)
        prev = ((s - 128 <= t) & (s - 128 > t - w)).astype(np.float32) / w
        cnt = np.minimum(t + 1, w).astype(np.float32)
        first = ((s <= t) & (s > t - w)).astype(np.float32) / cnt - (s == t).astype(np.float32)
        pm[g * 3 + 0], pm[g * 3 + 1], pm[g * 3 + 2] = cur, prev, first
    poolm = np.ascontiguousarray(pm.transpose(1, 0, 2).reshape(128, 12 * 128)).astype(bf)
    return ident, cmask, smask, onesbd, poolm


_CACHE = {}


def prepare(x, meta_tokens, lb_logits, norm1_g, w_in, b_f, head_norm_g, w_pool, pool_scale,
           w_branch_a, w_branch_b, w_out, norm2_g, w_up, conv_w, conv_b, w_down, final_norm_g):
    f32 = np.float32
    x = np.asarray(x, f32)
    B = x.shape[0]
    ncore = 8
    meta = np.asarray(meta_tokens, f32)

    def colize(v, n):
        return np.asarray(v, f32).reshape(n, 128).T

    cols = np.zeros((128, NCOLS), f32)
    cols[:, C_G1:C_G1 + 8] = colize(norm1_g[0], 8)
    cols[:, C_G2:C_G2 + 8] = colize(norm2_g[0], 8)
    cols[:, C_BF:C_BF + 8] = colize(b_f[0], 8)
    cols[:, C_L0:C_L0 + 8] = colize(lb_logits[0], 8)
    cols[:, C_L1:C_L1 + 8] = colize(lb_logits[1], 8)
    cols[:, C_HG:C_HG + 4] = colize(head_norm_g[0], 4)
    cols[:, C_PS:C_PS + 4] = colize(pool_scale[0], 4)
    cw = np.asarray(conv_w[0], f32)
    cols[:, C_W0:C_W0 + 22] = colize(cw[0], 22)
    cols[:, C_W1:C_W1 + 22] = colize(cw[1], 22)
    cols[:, C_W2:C_W2 + 22] = colize(cw[2], 22)
    cols[:, C_CB:C_CB + 22] = colize(conv_b[0], 22)
    cols[:16, C_PAD] = 1.0
    gF = np.ascontiguousarray(np.broadcast_to(np.asarray(final_norm_g, f32)[None, :], (128, D)))
    ident, cmask, smask, onesbd, poolm = _consts()
    shared = dict(
        w_in=np.ascontiguousarray(np.asarray(w_in, f32)[0]), w_up=np.ascontiguousarray(np.asarray(w_up, f32)[0]),
        w_down=np.ascontiguousarray(np.asarray(w_down, f32)[0]), w_out=np.ascontiguousarray(np.asarray(w_out, f32)[0]),
        w_a=np.ascontiguousarray(np.asarray(w_branch_a, f32)[0]), w_b=np.ascontiguousarray(np.asarray(w_branch_b, f32)[0]),
        w_pool=np.ascontiguousarray(np.asarray(w_pool, f32)[0].reshape(512, 128)),
        cols=cols, gF=gF, ident=ident, cmask=cmask, smask=smask, onesbd=onesbd, poolm=poolm)
    in_maps = []
    for c in range(ncore):
        xin = np.zeros((2, SEQP, D), f32)
        for s in range(2):
            xin[s, 0:16] = meta
            xin[s, 16:16 + 2048] = x[c * 2 + s]
        m = dict(shared)
        m["xin"] = xin.reshape(2 * SEQP, D)
        in_maps.append(m)
    return in_maps


def kernel(**inputs):
    f32 = np.float32
    ncore = 8
    in_maps = prepare(**inputs)
    if "nc" not in _CACHE:
        _CACHE["nc"] = build_program()
    nc = _CACHE["nc"]
    res = run_bass_kernel_spmd(nc, in_maps, core_ids=list(range(ncore)))
    outs = [np.asarray(r["out"], f32).reshape(2, 2048, D) for r in res.results]
    return np.concatenate(outs, axis=0)
```

```python
import numpy as np
import ml_dtypes
from contextlib import ExitStack
import concourse.bass as bass
import concourse.mybir as mybir
from concourse.bass_utils import run_bass_kernel_spmd

F32 = mybir.dt.float32
BF16 = mybir.dt.bfloat16
AF = mybir.ActivationFunctionType
ALU = mybir.AluOpType

D = 1024
NBS = 17
NBC = 2 * NBS
SEQP = NBS * 128
EPS = 1e-6
NS = 6
USE_SCRATCH = True
import os
HG_PIPE = os.environ.get('HG_PIPE', '1') == '1'
INTERLEAVE = os.environ.get('INTERLEAVE', '0') == '1'


class Slot:
    __slots__ = ("w", "r")

    def __init__(self):
        self.w = None
        self.r = {}


class Buf:
    def __init__(self, name, ap):
        self.name = name
        self.ap = ap
        self.slots = [Slot()]
        self.dsem = None
        self.dcnt = 0

    def __getitem__(self, k):
        return self.ap[k]


class Eng:
    def __init__(self, name):
        self.name = name
        self.key = "E_" + name
        self.sem = None
        self.cnt = 0
        self.ops = []
        self.seen = {}


class K:
    def __init__(self, nc, es):
        self.nc = nc
        self.es = es
        self.sems = {}
        self.engs = {}
        for name in ("pe", "act", "dve", "pool", "sp"):
            e = Eng(name)
            self.engs[name] = e
            if name != "sp":
                e.sem = es.enter_context(nc.semaphore("s_" + name))
                self.sems[e.key] = e.sem
        self.psum_banks = []
        self.psum_rr = 0
        self.ninst = 0
        self.nwaits = 0
        self.stage = 'init'
        self.labels = {n: [] for n in ('pe', 'act', 'dve', 'pool', 'sp')}

    def sb(self, name, shape, dt):
        t = self.es.enter_context(self.nc.sbuf_tensor("sb_" + name, list(shape), dt))
        return Buf(name, t[:])

    def alias(self, name, parent, ap):
        b = Buf(name, ap)
        b.slots = parent.slots
        return b

    def pseudo(self, name):
        return Buf(name, None)

    def psum_init(self):
        for i in range(8):
            t = self.es.enter_context(self.nc.psum_tensor("psb%d" % i, [128, 512], F32))
            self.psum_banks.append(Buf("psb%d" % i, t[:]))

    def psum(self):
        b = self.psum_banks[self.psum_rr % 8]
        self.psum_rr += 1
        return b

    def _deps(self, eng, reads, writes):
        need = {}

        def add(tok):
            if tok is None:
                return
            k, v = tok
            if need.get(k, 0) < v:
                need[k] = v
        for b in reads:
            for s in b.slots:
                add(s.w)
        for b in writes:
            for s in b.slots:
                add(s.w)
                for k, v in s.r.items():
                    add((k, v))
        out = []
        for k, v in need.items():
            if eng.name == "pe" and k == eng.key:
                continue
            if eng.seen.get(k, 0) >= v:
                continue
            eng.seen[k] = v
            out.append((k, v))
        return out

    def _mark(self, tok, reads, writes):
        k, v = tok
        for b in writes:
            for s in b.slots:
                s.w = tok
                s.r = {}
        for b in reads:
            for s in b.slots:
                if s.r.get(k, 0) < v:
                    s.r[k] = v

    def op(self, engname, fn, reads=(), writes=(), signal=True):
        eng = self.engs[engname]
        deps = self._deps(eng, reads, writes)
        for k, v in deps:
            if k == eng.key:
                assert v <= eng.cnt, (engname, v, eng.cnt)
        if signal:
            eng.cnt += 1
            tok = (eng.key, eng.cnt)
        else:
            tok = (eng.key, eng.cnt + 1)
        self._mark(tok, reads, writes)
        self.nwaits += len(deps)
        self.ninst += 1
        eng.ops.append((deps, fn, signal))
        self.labels[engname].append(self.stage)

    def dma(self, qname, fn, buf, reads=(), writes=()):
        eng = self.engs[qname]
        deps = self._deps(eng, reads, writes)
        kind = "sw" if qname == "pool" else "hw"
        if not hasattr(buf, "dsems"):
            buf.dsems = {}
        if kind not in buf.dsems:
            sem = self.es.enter_context(self.nc.semaphore("d%s_%s" % (kind, buf.name)))
            buf.dsems[kind] = [sem, 0]
            self.sems["D%s_%s" % (kind, buf.name)] = sem
        buf.dsems[kind][1] += 16
        tok = ("D%s_%s" % (kind, buf.name), buf.dsems[kind][1])
        self._mark(tok, reads, writes)
        self.nwaits += len(deps)
        self.ninst += 1
        eng.ops.append((deps, fn, ("dma", buf.dsems[kind][0])))
        self.labels[qname].append(self.stage)

    def final_wait(self, qname, bufs):
        eng = self.engs[qname]
        deps = self._deps(eng, (), bufs)
        eng.ops.append((deps, None, False))

    def run(self):
        nc = self.nc
        K_ = self

        def replay(name):
            def body(e):
                eng = K_.engs[name]
                for deps, fn, signal in eng.ops:
                    for k, v in deps:
                        e.wait_ge(K_.sems[k], v)
                    if fn is None:
                        continue
                    inst = fn(e)
                    if signal is True:
                        inst.then_inc(eng.sem, 1)
                    elif signal is False:
                        pass
                    else:
                        inst.then_inc(signal[1], 16)
            return body
        with nc.Block() as block:
            block.tensor(replay("pe"))
            block.scalar(replay("act"))
            block.vector(replay("dve"))
            block.gpsimd(replay("pool"))
            block.sync(replay("sp"))


C_G1, C_G2, C_BF, C_L0, C_L1 = 0, 8, 16, 24, 32
C_HG, C_PS = 40, 44
C_W0, C_W1, C_W2, C_CB = 48, 70, 92, 114
C_PAD = 136
NCOLS = 144
HC_HBF, HC_A1, HC_NA1, HC_B1, HC_PADH, HC_HALF = 0, 8, 16, 24, 32, 33


def tiles_of_core():
    t = []
    b = 0
    while b < NBC:
        nb = min(4, NBC - b)
        t.append((b, nb))
        b += nb
    return t


def weight_schedule():
    s = []
    extra = {0: [("i0", 2048)], 1: [("i1", 2304)], 2: [("og0", 2560), ("og1", 2816)], 3: [("pl0", 3072), ("pl1", 3328)]}
    for hp in range(4):
        s.append(("q%d" % hp, "w_in", hp * 256, None, 8, 256))
        s.append(("f%d" % hp, "w_in", 1024 + hp * 256, None, 8, 256))
        if INTERLEAVE:
            for nm_, c_ in extra[hp]:
                s.append((nm_, "w_in", c_, None, 8, 256))
    if not INTERLEAVE:
        for hp in range(4):
            for nm_, c_ in extra[hp]:
                s.append((nm_, "w_in", c_, None, 8, 256))
    for hf in range(2):
        s.append(("wa%d" % hf, "w_a", hf * 512, None, 4, 512))
        s.append(("wb%d" % hf, "w_b", hf * 512, None, 4, 512))
        for i in range(2):
            up = hf * 2 + i
            s.append(("ga%d" % up, "w_in", 3584 + up * 256, None, 8, 256))
            s.append(("gb%d" % up, "w_in", 4608 + up * 256, None, 8, 256))
    for q in range(4):
        s.append(("wo%d" % q, "w_out", q * 256, None, 8, 256))
    for mp in range(11):
        s.append(("uu%d" % mp, "w_up", mp * 256, None, 8, 256))
        s.append(("uv%d" % mp, "w_up", 2816 + mp * 256, None, 8, 256))
    for q in range(4):
        for mg in range(3):
            nm = 8 if mg < 2 else 6
            s.append(("wd%d_%d" % (q, mg), "w_down", q * 256, mg * 8, nm, 256))
    return s


def build_program(debug=False, ntiles=None):
    nc = bass.Bass("TRN2", target_bir_lowering=False)
    dr = {}
    dumps = {}

    def din(name, shape, dt=F32):
        dr[name] = nc.dram_tensor(name, list(shape), dt, kind="ExternalInput").ap()
        return dr[name]

    xin = din("xin", [NBC * 128, D])
    din("w_in", [D, 5632])
    din("w_up", [D, 5632])
    din("w_down", [2816, D])
    din("w_out", [D, D])
    din("w_a", [512, D])
    din("w_b", [512, D])
    w_pool = din("w_pool", [512, 128])
    cols_d = din("cols", [128, NCOLS])
    gF_d = din("gF", [128, D])
    ident_d = din("ident", [128, 128], BF16)
    cmask_d = din("cmask", [128, 512], mybir.dt.uint32)
    smask_d = din("smask", [128, 512])
    onesbd_d = din("onesbd", [128, 128], BF16)
    poolm_d = din("poolm", [128, 12 * 128], BF16)
    out = nc.dram_tensor("out", [2 * 2048, D], F32, kind="ExternalOutput").ap()
    sched = weight_schedule()
    NW = len(sched)
    scr = nc.dram_tensor("wscr", [NW, 128, 2048], BF16, kind="Internal").ap()

    wsrc = {
        "w_in": dr["w_in"].rearrange("(kc p) c -> p kc c", p=128),
        "w_up": dr["w_up"].rearrange("(kc p) c -> p kc c", p=128),
        "w_out": dr["w_out"].rearrange("(kc p) c -> p kc c", p=128),
        "w_a": dr["w_a"].rearrange("(kc p) c -> p kc c", p=128),
        "w_b": dr["w_b"].rearrange("(kc p) c -> p kc c", p=128),
        "w_down": dr["w_down"].rearrange("(m p) c -> p m c", p=128),
    }

    with ExitStack() as es:
        k = K(nc, es)
        k.psum_init()
        cols = k.sb("cols", [128, NCOLS], F32)
        gF = k.sb("gF", [128, D], F32)
        ident = k.sb("ident", [128, 128], BF16)
        cmask = k.sb("cmask", [128, 512], mybir.dt.uint32)
        smask = k.sb("smask", [128, 512], F32)
        onesbd = k.sb("onesbd", [128, 128], BF16)
        poolm = k.sb("poolm", [128, 12, 128], BF16)
        wpool = k.sb("wpool", [128, 4, 128], BF16)
        lbt = k.sb("lbt", [128, 8], F32)
        omlt = k.sb("omlt", [128, 8], F32)
        nomlt = k.sb("nomlt", [128, 8], F32)
        ldiff = k.sb("ldiff", [128, 8], F32)
        for b_, d_ in ((cols, cols_d), (gF, gF_d), (ident, ident_d), (cmask, cmask_d), (smask, smask_d), (onesbd, onesbd_d)):
            k.dma("sp", lambda e, b_=b_, d_=d_: e.dma_start(out=b_[:], in_=d_), b_, writes=[b_])
        k.dma("sp", lambda e: e.dma_start(out=poolm[:], in_=poolm_d.rearrange("p (m t) -> p m t", t=128)), poolm, writes=[poolm])
        k.dma("pool", lambda e: e.dma_start(out=wpool[:], in_=w_pool.rearrange("(g c) d -> c g d", c=128)), wpool, writes=[wpool])
        k.op("dve", lambda e: e.tensor_tensor(out=ldiff[:], in0=cols[:, C_L0:C_L0 + 8], in1=cols[:, C_L1:C_L1 + 8], op=ALU.subtract), reads=[cols], writes=[ldiff])
        k.op("act", lambda e: e.activation(out=lbt[:], in_=ldiff[:], func=AF.Sigmoid), reads=[ldiff], writes=[lbt])
        k.op("act", lambda e: e.activation(out=omlt[:], in_=ldiff[:], func=AF.Sigmoid, scale=-1.0), reads=[ldiff], writes=[omlt])
        k.op("dve", lambda e: e.tensor_scalar(out=nomlt[:], in0=omlt[:], scalar1=-1.0, scalar2=None, op0=ALU.mult), reads=[omlt], writes=[nomlt])
        hcol = k.sb("hcol", [128, 40], F32)
        epsc = k.sb("epsc", [128, 1], F32)
        k.op("pool", lambda e: e.memset(epsc[:], EPS), writes=[epsc])
        k.op("dve", lambda e: e.tensor_scalar(out=hcol[:, HC_HBF:HC_HBF + 8], in0=cols[:, C_BF:C_BF + 8], scalar1=0.5, scalar2=None, op0=ALU.mult), reads=[cols], writes=[hcol])
        k.op("dve", lambda e: e.tensor_scalar(out=hcol[:, HC_A1:HC_A1 + 8], in0=omlt[:], scalar1=0.5, scalar2=None, op0=ALU.mult), reads=[omlt, hcol], writes=[hcol])
        k.op("dve", lambda e: e.tensor_scalar(out=hcol[:, HC_NA1:HC_NA1 + 8], in0=omlt[:], scalar1=-0.5, scalar2=None, op0=ALU.mult), reads=[omlt, hcol], writes=[hcol])
        k.op("dve", lambda e: e.tensor_scalar(out=hcol[:, HC_B1:HC_B1 + 8], in0=lbt[:], scalar1=0.5, scalar2=0.5, op0=ALU.mult, op1=ALU.add), reads=[lbt, hcol], writes=[hcol])
        k.op("dve", lambda e: e.tensor_scalar(out=hcol[:, HC_PADH:HC_PADH + 1], in0=cols[:, C_PAD:C_PAD + 1], scalar1=0.5, scalar2=None, op0=ALU.mult), reads=[cols, hcol], writes=[hcol])
        k.op("pool", lambda e: e.memset(hcol[:, HC_HALF:HC_HALF + 1], 0.5), reads=[hcol], writes=[hcol])

        S = k.sb("S", [128, 8, 64], F32)
        Sb = [[k.sb("Sb%d_%d" % (p_, i), [128, 8, 64], BF16) for i in range(2)] for p_ in range(2)]
        Stmp = k.sb("Stmp", [128, 8, 64], F32)
        halo = k.sb("halo", [128, 22, 2], F32)
        Mt = k.sb("Mt", [128, 8, 8], F32)
        dSt = k.sb("dSt", [128, 8, 8], F32)
        Et = k.sb("Et", [128, 8, 8], F32)
        ptok = [k.sb("ptok%d" % i, [128, 4, 512], BF16) for i in range(2)]
        k.op("pool", lambda e: e.memset(halo[:], 0.0), writes=[halo])

        hbuf = [k.sb("h%d" % i, [128, 4, D], F32) for i in range(2)]
        junk = k.sb("junk", [128, D], BF16)
        ss1, rs1, ss2, rs2, ssF, rsF = [k.sb(n_, [128, 4], F32) for n_ in ("ss1", "rs1", "ss2", "rs2", "ssF", "rsF")]
        ztok = k.sb("ztok", [128, 4, D], BF16)
        zf = [k.sb("zf%d" % i, [128, 512], BF16) for i in range(8)]
        qt = [k.sb("qt%d" % i, [128, 512], BF16) for i in range(8)]
        kt = [k.sb("kt%d" % i, [128, 512], BF16) for i in range(8)]
        kT = [k.sb("kT%d" % i, [128, 1024], BF16) for i in range(4)]
        z2f = [k.sb("z2f%d" % i, [128, 512], BF16) for i in range(8)]
        gg = qt + kt + [k.alias("gg%d" % (16 + i), kT[i // 2], kT[i // 2].ap[:, (i % 2) * 512:(i % 2 + 1) * 512]) for i in range(6)]
        vtok = k.sb("vtok", [128, 4, 512], BF16)
        sog = k.sb("sog", [128, 4, 512], BF16)
        oa = k.sb("oa", [128, 4, 512], BF16)
        ob = k.alias("ob", vtok, vtok.ap)
        dfeat = k.sb("dfeat", [128, 4, 512], BF16)
        mix = [k.alias("mix%d" % i, sog, sog.ap[:, i, :]) for i in range(4)] + [k.sb("mix%d" % i, [128, 512], BF16) for i in range(4, 8)]
        qs = [k.sb("qs%d" % i, [128, 512], F32) for i in range(2)]
        th = [k.sb("th%d" % i, [128, 512], F32) for i in range(2)]
        lf = [k.sb("lf%d" % i, [128, 512], F32) for i in range(2)]
        kk = [k.sb("kk%d" % i, [128, 512], F32) for i in range(2)]
        aa = k.sb("aa", [128, 512], F32)
        ar = k.sb("ar", [128, 512], F32)
        ep = k.sb("ep", [128, 512], F32)
        en = k.sb("en", [128, 512], F32)
        msk = [[k.sb("msk%d_%d" % (p_, i), [128, 512], BF16) for i in range(2)] for p_ in range(2)]
        for mm_ in msk:
            for m_ in mm_:
                k.op("pool", lambda e, m_=m_: e.memset(m_[:], 0.0), writes=[m_])
        osq = k.sb("osq", [128, 512], BF16)
        ors = k.sb("ors", [128, 512], F32)
        on = k.sb("on", [128, 512], F32)
        sga = k.sb("sga", [128, 512], BF16)
        sgb = k.sb("sgb", [128, 512], BF16)
        t1 = k.alias("t1", dfeat, dfeat.ap[:, 0:2, :].rearrange("p a b -> p (a b)").bitcast(F32))
        t2 = k.alias("t2", dfeat, dfeat.ap[:, 2:4, :].rearrange("p a b -> p (a b)").bitcast(F32))
        ub = [k.sb("ub%d" % i, [128, 516], F32) for i in range(2)]
        c0 = [k.sb("c0_%d" % i, [128, 512], F32) for i in range(2)]
        c2 = [k.sb("c2_%d" % i, [128, 512], F32) for i in range(2)]
        ot = k.sb("ot", [128, D], F32)
        c1 = [k.alias("c1_%d" % i, ot, ot.ap[:, i * 512:(i + 1) * 512]) for i in range(2)]
        ring = [k.sb("wr%d" % i, [128, 2048], BF16) for i in range(NS)]
        scrb = [k.pseudo("scr%d" % i) for i in range(NW)]

        tiles = tiles_of_core()
        if ntiles is not None:
            tiles = tiles[:ntiles]

        def dump(name, buf, ap=None):
            if not debug or name in dumps:
                return
            src = buf.ap if ap is None else ap
            dt_ = src.dtype
            d = nc.dram_tensor('dbg_' + name, list(src.shape), dt_, kind='ExternalOutput').ap()
            dumps[name] = d
            k.dma('sp', lambda e, d=d, src=src: e.dma_start(out=d, in_=src), buf, reads=[buf])
        glob = [(ti, wi) for ti in range(len(tiles)) for wi in range(NW)]
        wstate = dict(emitted=0, next=0)
        loaded = {}

        def emit_load(gi):
            ti, wi = glob[gi]
            key, which, a, b, nk, ncol = sched[wi]
            slot = ring[gi % NS]
            view = slot.ap[:, 0:nk * ncol].rearrange("p (k c) -> p k c", c=ncol)
            if ti == 0 or not USE_SCRATCH:
                if which == "w_down":
                    src = wsrc[which][:, b:b + nk, a:a + ncol]
                else:
                    src = wsrc[which][:, :, a:a + ncol]
                k.dma("pool", lambda e, view=view, src=src: e.dma_start(out=view, in_=src), slot, writes=[slot])
                if USE_SCRATCH:
                    k.dma("sp", lambda e, slot=slot, wi=wi: e.dma_start(out=scr[wi], in_=slot[:]), slot, reads=[slot], writes=[scrb[wi]])
            else:
                k.dma("sp", lambda e, slot=slot, wi=wi: e.dma_start(out=slot[:], in_=scr[wi]), slot, reads=[scrb[wi]], writes=[slot])
            loaded[gi] = (slot, view)

        released = set()
        cur_gi = {}

        def pump():
            while wstate["emitted"] < len(glob):
                g_ = wstate["emitted"]
                if g_ - NS >= 0 and (g_ - NS) not in released:
                    break
                emit_load(g_)
                wstate["emitted"] += 1

        def wget(key):
            gi = wstate["next"]
            wstate["next"] += 1
            assert sched[glob[gi][1]][0] == key, (sched[glob[gi][1]][0], key)
            pump()
            assert gi in loaded, ("weight ring too small", key)
            slot, view = loaded.pop(gi)
            cur_gi[slot.name] = gi
            return slot, view

        def wrel(*slots):
            for sl in slots:
                released.add(cur_gi.pop(sl.name))
            pump()

        def mm(out_, lhsT, rhs, start, stop):
            return lambda e: e.matmul(out_, lhsT=lhsT, rhs=rhs, start=start, stop=stop)

        def rstd_from_ss(ssb, rsb, n, inv):
            k.op("act", lambda e: e.activation(out=rsb[:, 0:n], in_=ssb[:, 0:n], func=AF.Ln, scale=inv, bias=epsc[:, 0:1]), reads=[ssb, epsc], writes=[rsb])
            k.op("act", lambda e: e.activation(out=rsb[:, 0:n], in_=rsb[:, 0:n], func=AF.Exp, scale=-0.5), reads=[rsb], writes=[rsb])

        def norm_part_a(hb, nb, ssb, rsb):
            for b in range(nb):
                k.op("act", lambda e, b=b: e.activation(out=junk[:], in_=hb[:, b, :], func=AF.Square, accum_out=ssb[:, b:b + 1]),
                     reads=[hb], writes=[junk, ssb])
            rstd_from_ss(ssb, rsb, nb, 1.0 / D)
            for b in range(nb):
                k.op("dve", lambda e, b=b: e.tensor_scalar(out=ztok[:, b, :], in0=hb[:, b, :], scalar1=rsb[:, b:b + 1], scalar2=None, op0=ALU.mult),
                     reads=[hb, rsb], writes=[ztok])

        def norm_part_b(nb, TT, gcol, zdst):
            for kc in range(8):
                pb = k.psum()
                pv = pb.ap.bitcast(BF16)
                for b in range(nb):
                    k.op("pe", lambda e, b=b, kc=kc, pv=pv: e.transpose(out=pv[:, b * 128:(b + 1) * 128], in_=ztok[:, b, kc * 128:(kc + 1) * 128], identity=ident[:]),
                         reads=[ztok, ident], writes=[pb], signal=(b == nb - 1))
                if kc % 2 == 0:
                    k.op("act", lambda e, kc=kc, pv=pv: e.activation(out=zdst[kc][:, 0:TT], in_=pv[:, 0:TT], func=AF.Identity, scale=cols[:, gcol + kc:gcol + kc + 1]),
                         reads=[pb, cols], writes=[zdst[kc]])
                else:
                    k.op("dve", lambda e, kc=kc, pv=pv: e.tensor_scalar(out=zdst[kc][:, 0:TT], in0=pv[:, 0:TT], scalar1=cols[:, gcol + kc:gcol + kc + 1], scalar2=None, op0=ALU.mult),
                         reads=[pb, cols], writes=[zdst[kc]])

        def feat_mm(pb, wslot, wview, col0, zsrc, TT):
            for kc in range(8):
                k.op("pe", mm(pb[:, 0:TT], wview[:, kc, col0:col0 + 128], zsrc[kc][:, 0:TT], kc == 0, kc == 7),
                     reads=[zsrc[kc], wslot], writes=[pb], signal=(kc == 7))

        def load_h(ti):
            gb0, nb = tiles[ti]
            hb = hbuf[ti % 2]
            k.stage = 't%d_s0' % ti
            k.dma("sp", lambda e: e.dma_start(out=hb[:, 0:nb, :], in_=xin.rearrange("(b p) d -> p b d", p=128)[:, gb0:gb0 + nb, :]), hb, writes=[hb])

        def norm1_a(ti):
            gb0, nb = tiles[ti]
            k.stage = 't%d_s1norm' % ti
            norm_part_a(hbuf[ti % 2], nb, ss1, rs1)

        def norm1_b(ti):
            gb0, nb = tiles[ti]
            k.stage = 't%d_s1norm' % ti
            norm_part_b(nb, nb * 128, C_G1, zf)

        def mixer(ti):
            gb0, nb = tiles[ti]
            h = hbuf[ti % 2]
            TT = nb * 128
            ncn = nb * 2
            pt_cur = ptok[ti % 2]
            pt_prev = ptok[(ti + 1) % 2]
            for i_ in range(8):
                dump('zf%d' % i_, zf[i_])
            def do_v(i):
                k.stage = 't%d_s2b_v' % ti
                slot, view = wget("i%d" % i)
                for b in range(nb):
                    pb = k.psum()
                    for kc in range(8):
                        k.op("pe", mm(pb[:, 0:256], zf[kc][:, b * 128:(b + 1) * 128], view[:, kc, :], kc == 0, kc == 7),
                             reads=[zf[kc], slot], writes=[pb], signal=(kc == 7))
                    k.op("dve", lambda e, pb=pb, b=b, i=i: e.tensor_copy(out=vtok[:, b, i * 256:(i + 1) * 256], in_=pb[:, 0:256]), reads=[pb], writes=[vtok])
                wrel(slot)

            def do_og(i):
                k.stage = 't%d_s2c_og' % ti
                slot, view = wget("og%d" % i)
                for jj in range(2):
                    u = i * 2 + jj
                    pb = k.psum()
                    feat_mm(pb, slot, view, jj * 128, zf, TT)
                    k.op("act", lambda e, pb=pb, u=u: e.activation(out=sog[:, u, 0:TT], in_=pb[:, 0:TT], func=AF.Silu), reads=[pb], writes=[sog])
                wrel(slot)

            def do_pl(i):
                k.stage = 't%d_s2d_p' % ti
                slot, view = wget("pl%d" % i)
                for b in range(nb):
                    pb = k.psum()
                    for kc in range(8):
                        k.op("pe", mm(pb[:, 0:256], zf[kc][:, b * 128:(b + 1) * 128], view[:, kc, :], kc == 0, kc == 7),
                             reads=[zf[kc], slot], writes=[pb], signal=(kc == 7))
                    k.op("pool", lambda e, pb=pb, b=b, i=i: None, reads=[], writes=[]) if False else None
                    k.op("dve", lambda e, pb=pb, b=b, i=i: e.tensor_copy(out=pt_cur[:, b, i * 256:(i + 1) * 256], in_=pb[:, 0:256]), reads=[pb], writes=[pt_cur])
                wrel(slot)

            extra = {0: [lambda: do_v(0)], 1: [lambda: do_v(1)], 2: [lambda: do_og(0), lambda: do_og(1)], 3: [lambda: do_pl(0), lambda: do_pl(1)]}
            for hp in range(4):
                k.stage = 't%d_s2a_qf' % ti
                qslot, qview = wget("q%d" % hp)
                fslot, fview = wget("f%d" % hp)
                pqs, pfs = [], []
                for jj in range(2):
                    pq = k.psum()
                    feat_mm(pq, qslot, qview, jj * 128, zf, TT)
                    pf = k.psum()
                    feat_mm(pf, fslot, fview, jj * 128, zf, TT)
                    pqs.append(pq)
                    pfs.append(pf)
                wrel(qslot, fslot)
                if INTERLEAVE:
                    for f_ in extra[hp]:
                        f_()
                k.stage = 't%d_s2a_qf' % ti
                for jj in range(2):
                    k.op("act", lambda e, pq=pqs[jj], q_=qs[jj]: e.activation(out=q_[:, 0:TT], in_=pq[:, 0:TT], func=AF.Silu), reads=[pqs[jj]], writes=[qs[jj]])
                for jj in range(2):
                    j = hp * 2 + jj
                    k.op("act", lambda e, pf=pfs[jj], t_=th[jj], j=j: e.activation(out=t_[:, 0:TT], in_=pf[:, 0:TT], func=AF.Tanh, scale=0.5, bias=hcol[:, HC_HBF + j:HC_HBF + j + 1]),
                         reads=[pfs[jj], hcol], writes=[th[jj]])
                for jj in range(2):
                    j = hp * 2 + jj
                    k.op("act", lambda e, t_=th[jj], l_=lf[jj], j=j: e.activation(out=l_[:, 0:TT], in_=t_[:, 0:TT], func=AF.Ln, scale=hcol[:, HC_A1 + j:HC_A1 + j + 1], bias=hcol[:, HC_B1 + j:HC_B1 + j + 1]),
                         reads=[th[jj], hcol], writes=[lf[jj]])
                    k.op("pool", lambda e, t_=th[jj], k_=kk[jj], j=j: e.tensor_scalar(out=k_[:, 0:TT], in0=t_[:, 0:TT], scalar1=hcol[:, HC_NA1 + j:HC_NA1 + j + 1], scalar2=hcol[:, HC_A1 + j:HC_A1 + j + 1], op0=ALU.mult, op1=ALU.add),
                         reads=[th[jj], hcol], writes=[kk[jj]])
                for jj in range(2):
                    j = hp * 2 + jj
                    k.op("dve", lambda e, l_=lf[jj]: e.tensor_tensor_scan(out=aa[:, 0:TT], data0=smask[:, 0:TT], data1=l_[:, 0:TT], initial=0.0, op0=ALU.mult, op1=ALU.add),
                         reads=[smask, lf[jj]], writes=[aa])
                    a3 = aa.ap.rearrange("p (c t) -> p c t", t=64)
                    ar3 = ar.ap.rearrange("p (c t) -> p c t", t=64)
                    ep3 = ep.ap.rearrange("p (c t) -> p c t", t=64)
                    k.op("dve", lambda e, a3=a3, ar3=ar3: e.tensor_tensor(out=ar3[:, 0:ncn, :], in0=a3[:, 0:ncn, :], in1=a3[:, 0:ncn, 31:32].to_broadcast([128, ncn, 64]), op=ALU.subtract),
                         reads=[aa], writes=[ar])
                    k.op("act", lambda e, a3=a3, j=j: e.activation(out=Mt[:, j, 0:ncn], in_=a3[:, 0:ncn, 31], func=AF.Exp), reads=[aa], writes=[Mt])
                    k.op("act", lambda e, a3=a3, j=j: e.activation(out=dSt[:, j, 0:ncn], in_=a3[:, 0:ncn, 63], func=AF.Exp), reads=[aa], writes=[dSt])
                    k.op("act", lambda e: e.activation(out=ep[:, 0:TT], in_=ar[:, 0:TT], func=AF.Exp), reads=[ar], writes=[ep])
                    k.op("act", lambda e: e.activation(out=en[:, 0:TT], in_=ar[:, 0:TT], func=AF.Exp, scale=-1.0), reads=[ar], writes=[en])
                    k.op("pool", lambda e, ep3=ep3, j=j: e.tensor_copy(out=Et[:, j, 0:ncn], in_=ep3[:, 0:ncn, 63]), reads=[ep], writes=[Et])
                    k.op("dve", lambda e, j=j, q_=qs[jj]: e.tensor_tensor(out=qt[j][:, 0:TT], in0=q_[:, 0:TT], in1=ep[:, 0:TT], op=ALU.mult), reads=[qs[jj], ep], writes=[qt[j]])
                    k.op("pool", lambda e, j=j, k_=kk[jj]: e.tensor_tensor(out=kt[j][:, 0:TT], in0=k_[:, 0:TT], in1=en[:, 0:TT], op=ALU.mult), reads=[kk[jj], en], writes=[kt[j]])
            if not INTERLEAVE:
                for hp_ in range(4):
                    for f_ in extra[hp_]:
                        f_()
            for i_ in range(8):
                dump('qt%d' % i_, qt[i_])
                dump('kt%d' % i_, kt[i_])
            dump('Mt', Mt)
            dump('dSt', dSt)
            dump('Et', Et)
            dump('vtok', vtok)
            dump('sog', sog)
            dump('ptok', pt_cur)
            k.stage = 't%d_s4_pool' % ti
            for b in range(nb):
                gb = gb0 + b
                pb = k.psum()
                for g in range(4):
                    cur = pt_cur[:, b, g * 128:(g + 1) * 128]
                    if gb % NBS == 0:
                        k.op("pe", mm(pb[:, g * 128:(g + 1) * 128], cur, poolm[:, g * 3 + 2, :], True, True), reads=[pt_cur, poolm], writes=[pb], signal=(g == 3))
                    else:
                        if b == 0:
                            prevb, prev = pt_prev, pt_prev[:, 3, g * 128:(g + 1) * 128]
                        else:
                            prevb, prev = pt_cur, pt_cur[:, b - 1, g * 128:(g + 1) * 128]
                        k.op("pe", mm(pb[:, g * 128:(g + 1) * 128], cur, poolm[:, g * 3 + 0, :], True, False), reads=[pt_cur, poolm], writes=[pb], signal=False)
                        k.op("pe", mm(pb[:, g * 128:(g + 1) * 128], prev, poolm[:, g * 3 + 1, :], False, True), reads=[prevb, poolm], writes=[pb], signal=(g == 3))
                k.op("dve", lambda e, pb=pb, b=b: e.tensor_copy(out=dfeat[:, :, b * 128:(b + 1) * 128], in_=pb.ap.rearrange("p (g t) -> p g t", t=128)),
                     reads=[pb], writes=[dfeat])
            k.stage = 't%d_s3_hgrn' % ti
            hg_state = {}

            def hg_a(b):
                gb = gb0 + b
                ca, cb = 2 * b, 2 * b + 1
                Sbb = Sb[b % 2]
                mskb = msk[b % 2]
                if gb % NBS == 0:
                    k.op("dve", lambda e: e.memset(S[:], 0.0), writes=[S])
                k.op("dve", lambda e, ca=ca, Sbb=Sbb: e.tensor_tensor(out=Sbb[0][:], in0=S[:], in1=Mt[:, :, ca:ca + 1].to_broadcast([128, 8, 64]), op=ALU.mult),
                     reads=[S, Mt], writes=[Sbb[0]])
                pb = k.psum()
                pv = pb.ap.bitcast(BF16)
                for j in range(8):
                    k.op("pe", lambda e, j=j, b=b, pv=pv: e.transpose(out=pv[:, j * 128:(j + 1) * 128], in_=kt[j][:, b * 128:(b + 1) * 128], identity=ident[:]),
                         reads=[kt[j], ident], writes=[pb], signal=(j == 7))
                k.op("act", lambda e, pv=pv, b=b: e.activation(out=kT[b][:], in_=pv, func=AF.Copy), reads=[pb], writes=[kT[b]])
                for half in range(2):
                    pb = k.psum()
                    for jj in range(4):
                        j = half * 4 + jj
                        k.op("pe", mm(pb[:, jj * 128:(jj + 1) * 128], kt[j][:, b * 128:(b + 1) * 128], qt[j][:, b * 128:(b + 1) * 128], True, True),
                             reads=[kt[j], qt[j]], writes=[pb], signal=(jj == 3))
                    k.op("dve", lambda e, pb=pb, half=half, mskb=mskb: e.copy_predicated(out=mskb[half][:], mask=cmask[:], data=pb[:]),
                         reads=[pb, cmask, mskb[half]], writes=[mskb[half]])
                P = []
                for cc in range(2):
                    pp = k.psum()
                    P.append(pp)
                    r0 = cc * 64
                    for j in range(8):
                        k.op("pe", mm(pp[:, j * 64:(j + 1) * 64], kT[b][r0:r0 + 64, j * 128:(j + 1) * 128], vtok[r0:r0 + 64, b, j * 64:(j + 1) * 64], True, True),
                             reads=[kT[b], vtok], writes=[pp], signal=(j == 7))
                for cc, cidx in ((0, ca), (1, cb)):
                    pp = P[cc]
                    k.op("dve", lambda e, cidx=cidx: e.tensor_tensor(out=S[:], in0=S[:], in1=dSt[:, :, cidx:cidx + 1].to_broadcast([128, 8, 64]), op=ALU.mult),
                         reads=[S, dSt], writes=[S])
                    k.op("dve", lambda e, cidx=cidx, pp=pp: e.tensor_tensor(out=Stmp[:], in0=pp.ap.rearrange("p (j v) -> p j v", v=64),
                                                                          in1=Et[:, :, cidx:cidx + 1].to_broadcast([128, 8, 64]), op=ALU.mult),
                         reads=[pp, Et], writes=[Stmp])
                    k.op("dve", lambda e: e.tensor_tensor(out=S[:], in0=S[:], in1=Stmp[:], op=ALU.add), reads=[S, Stmp], writes=[S])
                    if cc == 0:
                        k.op("dve", lambda e, cb=cb, Sbb=Sbb: e.tensor_tensor(out=Sbb[1][:], in0=S[:], in1=Mt[:, :, cb:cb + 1].to_broadcast([128, 8, 64]), op=ALU.mult),
                             reads=[S, Mt], writes=[Sbb[1]])

            def hg_b(b):
                Sbb = Sb[b % 2]
                mskb = msk[b % 2]
                po = k.psum()
                for j in range(8):
                    r0 = (j % 2) * 64
                    cbase = (j // 2) * 128
                    half, jj = j // 4, j % 4
                    k.op("pe", mm(po[r0:r0 + 64, cbase:cbase + 128], vtok[:, b, j * 64:(j + 1) * 64], mskb[half][:, jj * 128:(jj + 1) * 128], True, False),
                         reads=[vtok, mskb[half]], writes=[po], signal=False)
                    k.op("pe", mm(po[r0:r0 + 64, cbase:cbase + 64], Sbb[0][:, j, :], qt[j][:, b * 128:b * 128 + 64], False, False),
                         reads=[Sbb[0], qt[j]], writes=[po], signal=False)
                    k.op("pe", mm(po[r0:r0 + 64, cbase + 64:cbase + 128], Sbb[1][:, j, :], qt[j][:, b * 128 + 64:b * 128 + 128], False, True),
                         reads=[Sbb[1], qt[j]], writes=[po], signal=(j == 7))
                k.op("act", lambda e, po=po: e.activation(out=osq[:], in_=po[:], func=AF.Square), reads=[po], writes=[osq])
                pss = k.psum()
                k.op("pe", mm(pss[:], onesbd[:], osq[:], True, True), reads=[onesbd, osq], writes=[pss])
                k.op("act", lambda e, pss=pss: e.activation(out=ors[:], in_=pss[:], func=AF.Ln, scale=1.0 / 64, bias=epsc[:, 0:1]), reads=[pss, epsc], writes=[ors])
                k.op("act", lambda e: e.activation(out=ors[:], in_=ors[:], func=AF.Exp, scale=-0.5), reads=[ors], writes=[ors])
                k.op("dve", lambda e, po=po: e.tensor_tensor(out=on[:], in0=po[:], in1=ors[:], op=ALU.mult), reads=[po, ors], writes=[on])
                k.op("pool", lambda e, b=b: e.tensor_tensor(out=on.ap.rearrange("p (j t) -> p j t", t=128), in0=on.ap.rearrange("p (j t) -> p j t", t=128),
                                                          in1=sog[:, :, b * 128:(b + 1) * 128], op=ALU.mult), reads=[on, sog], writes=[on])
                k.op("pool", lambda e, b=b: e.tensor_tensor(out=oa[:, :, b * 128:(b + 1) * 128], in0=on.ap.rearrange("p (j t) -> p j t", t=128),
                                                          in1=cols[:, C_HG:C_HG + 4].unsqueeze(2).to_broadcast([128, 4, 128]), op=ALU.mult),
                     reads=[on, cols], writes=[oa])

            if HG_PIPE:
                hg_a(0)
                for b in range(nb):
                    if b + 1 < nb:
                        hg_a(b + 1)
                    hg_b(b)
            else:
                for b in range(nb):
                    hg_a(b)
                    hg_b(b)
            dump('oa', oa)
            dump('S', S)
            k.stage = 't%d_s4_pool' % ti
            for g in range(4):
                pb = k.psum()
                k.op("pe", mm(pb[:, 0:TT], wpool[:, g, :], dfeat[:, g, 0:TT], True, True), reads=[wpool, dfeat], writes=[pb])
                k.op("dve", lambda e, pb=pb, g=g: e.tensor_scalar(out=ob[:, g, 0:TT], in0=pb[:, 0:TT], scalar1=cols[:, C_PS + g:C_PS + g + 1], scalar2=None, op0=ALU.mult),
                     reads=[pb, cols], writes=[ob])
            dump('ob', ob)
            k.stage = 't%d_s5_gate' % ti
            for hf in range(2):
                waslot, waview = wget("wa%d" % hf)
                wbslot, wbview = wget("wb%d" % hf)
                for i in range(2):
                    up = hf * 2 + i
                    gaslot, gaview = wget("ga%d" % up)
                    gbslot, gbview = wget("gb%d" % up)
                    for jj in range(2):
                        u = up * 2 + jj
                        for (gslot, gview, wslot_, wview_, src, sgt, tt) in ((gaslot, gaview, waslot, waview, oa, sga, t1), (gbslot, gbview, wbslot, wbview, ob, sgb, t2)):
                            pg = k.psum()
                            feat_mm(pg, gslot, gview, jj * 128, zf, TT)
                            k.op("act", lambda e, pg=pg, sgt=sgt: e.activation(out=sgt[:, 0:TT], in_=pg[:, 0:TT], func=AF.Tanh, scale=0.5), reads=[pg], writes=[sgt])
                            pa = k.psum()
                            cu = (u % 4) * 128
                            for kc in range(4):
                                k.op("pe", mm(pa[:, 0:TT], wview_[:, kc, cu:cu + 128], src[:, kc, 0:TT], kc == 0, kc == 3),
                                     reads=[wslot_, src], writes=[pa], signal=(kc == 3))
                            k.op("dve", lambda e, pa=pa, sgt=sgt, tt=tt: e.scalar_tensor_tensor(out=tt[:, 0:TT], in0=sgt[:, 0:TT], scalar=1.0, in1=pa[:, 0:TT], op0=ALU.add, op1=ALU.mult),
                                 reads=[pa, sgt], writes=[tt])
                        k.op("pool", lambda e, u=u: e.tensor_tensor(out=mix[u][:, 0:TT], in0=t1[:, 0:TT], in1=t2[:, 0:TT], op=ALU.add), reads=[t1, t2], writes=[mix[u]])
                    wrel(gaslot, gbslot)
                wrel(waslot, wbslot)
            for i_ in range(8):
                dump('mix%d' % i_, mix[i_])
            k.stage = 't%d_s6_wout' % ti
            for q in range(4):
                slot, view = wget("wo%d" % q)
                for b in range(nb):
                    gb = gb0 + b
                    pb = k.psum()
                    for u in range(8):
                        k.op("pe", mm(pb[:, 0:256], mix[u][:, b * 128:(b + 1) * 128], view[:, u, :], u == 0, u == 7),
                             reads=[mix[u], slot], writes=[pb], signal=(u == 7))
                    sc_ = hcol[:, HC_PADH:HC_PADH + 1] if gb % NBS == NBS - 1 else hcol[:, HC_HALF:HC_HALF + 1]
                    k.op("dve", lambda e, pb=pb, b=b, q=q, sc_=sc_: e.scalar_tensor_tensor(out=h[:, b, q * 256:(q + 1) * 256], in0=pb[:, 0:256], scalar=sc_,
                                                                                          in1=h[:, b, q * 256:(q + 1) * 256], op0=ALU.mult, op1=ALU.add),
                         reads=[pb, h, hcol], writes=[h])
                wrel(slot)
            dump('h1', h)

        def ffn(ti, hooks):
            gb0, nb = tiles[ti]
            h = hbuf[ti % 2]
            TT = nb * 128
            k.stage = 't%d_s7norm' % ti
            norm_part_a(h, nb, ss2, rs2)
            norm_part_b(nb, TT, C_G2, z2f)
            k.stage = 't%d_s8_up' % ti
            for mp in range(11):
                if mp in hooks:
                    hooks[mp]()
                    k.stage = 't%d_s8_up' % ti
                uslot, uview = wget("uu%d" % mp)
                vslot, vview = wget("uv%d" % mp)
                for jj in range(2):
                    m = mp * 2 + jj
                    par = m % 2
                    pu = k.psum()
                    feat_mm(pu, uslot, uview, jj * 128, z2f, TT)
                    pvv = k.psum()
                    feat_mm(pvv, vslot, vview, jj * 128, z2f, TT)
                    ubm, c0m, c2m, c1m = ub[par], c0[par], c2[par], c1[par]
                    k.op("pool", lambda e, ubm=ubm, m=m: e.tensor_copy(out=ubm[:, 0:2], in_=halo[:, m, :]), reads=[halo], writes=[ubm])
                    k.op("act", lambda e, ubm=ubm, pu=pu: e.activation(out=ubm[:, 2:2 + TT], in_=pu[:, 0:TT], func=AF.Copy), reads=[pu], writes=[ubm])
                    k.op("pool", lambda e, ubm=ubm, m=m: e.tensor_copy(out=halo[:, m, :], in_=ubm[:, TT:TT + 2]), reads=[ubm], writes=[halo])
                    k.op("act", lambda e, ubm=ubm, c1m=c1m, m=m: e.activation(out=c1m[:, 0:TT], in_=ubm[:, 1:1 + TT], func=AF.Identity, scale=cols[:, C_W1 + m:C_W1 + m + 1],
                                                                             bias=cols[:, C_CB + m:C_CB + m + 1]),
                         reads=[ubm, cols], writes=[c1m])
                    k.op("pool", lambda e, ubm=ubm, c0m=c0m, m=m: e.tensor_scalar(out=c0m[:, 0:TT], in0=ubm[:, 0:TT], scalar1=cols[:, C_W0 + m:C_W0 + m + 1],
                                                                               scalar2=0.0, op0=ALU.mult, op1=ALU.add),
                         reads=[ubm, cols], writes=[c0m])
                    k.op("pool", lambda e, c0m=c0m, c1m=c1m: e.tensor_tensor(out=c0m[:, 0:TT], in0=c0m[:, 0:TT], in1=c1m[:, 0:TT], op=ALU.add),
                         reads=[c0m, c1m], writes=[c0m])
                    k.op("dve", lambda e, pu=pu, c0m=c0m, c2m=c2m, m=m: e.scalar_tensor_tensor(out=c2m[:, 0:TT], in0=pu[:, 0:TT], scalar=cols[:, C_W2 + m:C_W2 + m + 1],
                                                                                            in1=c0m[:, 0:TT], op0=ALU.mult, op1=ALU.add),
                         reads=[pu, c0m, cols], writes=[c2m])
                    k.op("act", lambda e, c2m=c2m: e.activation(out=c2m[:, 0:TT], in_=c2m[:, 0:TT], func=AF.Silu), reads=[c2m], writes=[c2m])
                    k.op("dve", lambda e, c2m=c2m, pvv=pvv, m=m: e.tensor_tensor(out=gg[m][:, 0:TT], in0=pvv[:, 0:TT], in1=c2m[:, 0:TT], op=ALU.mult),
                         reads=[pvv, c2m], writes=[gg[m]])
                wrel(uslot, vslot)
            for i_ in range(22):
                dump('gg%d' % i_, gg[i_])
            k.stage = 't%d_s9_down' % ti
            for q in range(4):
                if ('d', q) in hooks:
                    hooks[('d', q)]()
                    k.stage = 't%d_s9_down' % ti
                ws = [wget("wd%d_%d" % (q, mg)) for mg in range(3)]
                for b in range(nb):
                    pb = k.psum()
                    for m in range(22):
                        slot, view = ws[m // 8]
                        k.op("pe", mm(pb[:, 0:256], gg[m][:, b * 128:(b + 1) * 128], view[:, m % 8, :], m == 0, m == 21),
                             reads=[gg[m], slot], writes=[pb], signal=(m == 21))
                    k.op("dve", lambda e, pb=pb, b=b, q=q: e.tensor_tensor(out=h[:, b, q * 256:(q + 1) * 256], in0=pb[:, 0:256], in1=h[:, b, q * 256:(q + 1) * 256], op=ALU.add),
                         reads=[pb, h], writes=[h])
                wrel(*[w_[0] for w_ in ws])
            dump('h2', h)

        def final(ti):
            gb0, nb = tiles[ti]
            h = hbuf[ti % 2]
            k.stage = 't%d_s10_out' % ti
            for b in range(nb):
                k.op("act", lambda e, b=b: e.activation(out=junk[:], in_=h[:, b, :], func=AF.Square, accum_out=ssF[:, b:b + 1]), reads=[h], writes=[junk, ssF])
            rstd_from_ss(ssF, rsF, nb, 1.0 / D)
            for b in range(nb):
                gb = gb0 + b
                s_, j_ = gb // NBS, gb % NBS
                k.op("pool" if b % 2 else "dve", lambda e, b=b: e.tensor_scalar(out=ot[:], in0=h[:, b, :], scalar1=rsF[:, b:b + 1], scalar2=None, op0=ALU.mult),
                     reads=[h, rsF], writes=[ot]) if False else None
                k.op("dve", lambda e, b=b: e.scalar_tensor_tensor(out=ot[:], in0=h[:, b, :], scalar=rsF[:, b:b + 1], in1=gF[:], op0=ALU.mult, op1=ALU.mult),
                     reads=[h, rsF, gF], writes=[ot])
                if j_ == 0:
                    r0, r1, o0 = 16, 128, s_ * 2048
                elif j_ == NBS - 1:
                    r0, r1, o0 = 0, 16, s_ * 2048 + 2032
                else:
                    r0, r1, o0 = 0, 128, s_ * 2048 + j_ * 128 - 16
                k.dma("sp", lambda e, r0=r0, r1=r1, o0=o0: e.dma_start(out=out[o0:o0 + (r1 - r0), :], in_=ot[r0:r1, :]), ot, reads=[ot])

        NT = len(tiles)
        load_h(0)
        norm1_a(0)
        norm1_b(0)
        for ti in range(NT):
            mixer(ti)
            hooks = {}
            if ti + 1 < NT:
                load_h(ti + 1)
                hooks[2] = (lambda t=ti + 1: norm1_a(t))
                hooks[('d', 1)] = (lambda t=ti + 1: norm1_b(t))
            ffn(ti, hooks)
            final(ti)
        k.final_wait("sp", [ot])
        if debug:
            k.final_wait('sp', [Mt, dSt, Et, vtok, sog, ptok[0], oa, S, ob] + hbuf + zf + qt + kt + mix + gg)
        k.run()
        build_program.stats = (k.ninst, k.nwaits, {n: len(e.ops) for n, e in k.engs.items()})
        build_program.labels = k.labels
        build_program.sbuf_free = nc.sbuf_bytes_remaining
    return nc


def _consts():
    bf = ml_dtypes.bfloat16
    ident = np.eye(128, dtype=np.float32).astype(bf)
    s = np.arange(128)[:, None]
    t = np.arange(128)[None, :]
    cmask = np.ascontiguousarray(np.tile(((s // 64 == t // 64) & (s <= t)).astype(np.uint32), (1, 4)))
    smask = np.ones((128, 512), np.float32)
    smask[:, ::64] = 0.0
    onesbd = (s // 64 == t // 64).astype(np.float32).astype(bf)
    pm = np.zeros((12, 128, 128), np.float32)
    for g, w in enumerate((2, 4, 8, 16)):
        cur = ((s <= t) & (s > t - w)).astype(np.float32) / w - (s == t).astype(np.float32)
        prev = ((s - 128 <= t) & (s - 128 > t - w)).astype(np.float32) / w
        cnt = np.minimum(t + 1, w).astype(np.float32)
        first = ((s <= t) & (s > t - w)).astype(np.float32) / cnt - (s == t).astype(np.float32)
        pm[g * 3 + 0], pm[g * 3 + 1], pm[g * 3 + 2] = cur, prev, first
    poolm = np.ascontiguousarray(pm.transpose(1, 0, 2).reshape(128, 12 * 128)).astype(bf)
    return ident, cmask, smask, onesbd, poolm


_CACHE = {}


def prepare(x, meta_tokens, lb_logits, norm1_g, w_in, b_f, head_norm_g, w_pool, pool_scale,
           w_branch_a, w_branch_b, w_out, norm2_g, w_up, conv_w, conv_b, w_down, final_norm_g):
    f32 = np.float32
    x = np.asarray(x, f32)
    B = x.shape[0]
    ncore = 8
    meta = np.asarray(meta_tokens, f32)

    def colize(v, n):
        return np.asarray(v, f32).reshape(n, 128).T

    cols = np.zeros((128, NCOLS), f32)
    cols[:, C_G1:C_G1 + 8] = colize(norm1_g[0], 8)
    cols[:, C_G2:C_G2 + 8] = colize(norm2_g[0], 8)
    cols[:, C_BF:C_BF + 8] = colize(b_f[0], 8)
    cols[:, C_L0:C_L0 + 8] = colize(lb_logits[0], 8)
    cols[:, C_L1:C_L1 + 8] = colize(lb_logits[1], 8)
    cols[:, C_HG:C_HG + 4] = colize(head_norm_g[0], 4)
    cols[:, C_PS:C_PS + 4] = colize(pool_scale[0], 4)
    cw = np.asarray(conv_w[0], f32)
    cols[:, C_W0:C_W0 + 22] = colize(cw[0], 22)
    cols[:, C_W1:C_W1 + 22] = colize(cw[1], 22)
    cols[:, C_W2:C_W2 + 22] = colize(cw[2], 22)
    cols[:, C_CB:C_CB + 22] = colize(conv_b[0], 22)
    cols[:16, C_PAD] = 1.0
    gF = np.ascontiguousarray(np.broadcast_to(np.asarray(final_norm_g, f32)[None, :], (128, D)))
    ident, cmask, smask, onesbd, poolm = _consts()
    shared = dict(
        w_in=np.ascontiguousarray(np.asarray(w_in, f32)[0]), w_up=np.ascontiguousarray(np.asarray(w_up, f32)[0]),
        w_down=np.ascontiguousarray(np.asarray(w_down, f32)[0]), w_out=np.ascontiguousarray(np.asarray(w_out, f32)[0]),
        w_a=np.ascontiguousarray(np.asarray(w_branch_a, f32)[0]), w_b=np.ascontiguousarray(np.asarray(w_branch_b, f32)[0]),
        w_pool=np.ascontiguousarray(np.asarray(w_pool, f32)[0].reshape(512, 128)),
        cols=cols, gF=gF, ident=ident, cmask=cmask, smask=smask, onesbd=onesbd, poolm=poolm)
    in_maps = []
    for c in range(ncore):
        xin = np.zeros((2, SEQP, D), f32)
        for s in range(2):
            xin[s, 0:16] = meta
            xin[s, 16:16 + 2048] = x[c * 2 + s]
        m = dict(shared)
        m["xin"] = xin.reshape(2 * SEQP, D)
        in_maps.append(m)
    return in_maps


def kernel(**inputs):
    f32 = np.float32
    ncore = 8
    in_maps = prepare(**inputs)
    if "nc" not in _CACHE:
        _CACHE["nc"] = build_program()
    nc = _CACHE["nc"]
    res = run_bass_kernel_spmd(nc, in_maps, core_ids=list(range(ncore)))
    outs = [np.asarray(r["out"], f32).reshape(2, 2048, D) for r in res.results]
    return np.concatenate(outs, axis=0)
```

```python
import numpy as np
import ml_dtypes
from contextlib import ExitStack
import concourse.bass as bass
import concourse.mybir as mybir
from concourse.bass_utils import run_bass_kernel_spmd

F32 = mybir.dt.float32
BF16 = mybir.dt.bfloat16
AF = mybir.ActivationFunctionType
ALU = mybir.AluOpType

D = 1024
NBS = 17
NBC = 2 * NBS
SEQP = NBS * 128
EPS = 1e-6
NS = 6
USE_SCRATCH = True
import os
HG_PIPE = os.environ.get('HG_PIPE', '1') == '1'
INTERLEAVE = os.environ.get('INTERLEAVE', '0') == '1'


class Slot:
    __slots__ = ("w", "r")

    def __init__(self):
        self.w = None
        self.r = {}


class Buf:
    def __init__(self, name, ap):
        self.name = name
        self.ap = ap
        self.slots = [Slot()]
        self.dsem = None
        self.dcnt = 0

    def __getitem__(self, k):
        return self.ap[k]


class Eng:
    def __init__(self, name):
        self.name = name
        self.key = "E_" + name
        self.sem = None
        self.cnt = 0
        self.ops = []
        self.seen = {}


class K:
    def __init__(self, nc, es):
        self.nc = nc
        self.es = es
        self.sems = {}
        self.engs = {}
        for name in ("pe", "act", "dve", "pool", "sp"):
            e = Eng(name)
            self.engs[name] = e
            if name != "sp":
                e.sem = es.enter_context(nc.semaphore("s_" + name))
                self.sems[e.key] = e.sem
        self.psum_banks = []
        self.psum_rr = 0
        self.ninst = 0
        self.nwaits = 0
        self.stage = 'init'
        self.labels = {n: [] for n in ('pe', 'act', 'dve', 'pool', 'sp')}

    def sb(self, name, shape, dt):
        t = self.es.enter_context(self.nc.sbuf_tensor("sb_" + name, list(shape), dt))
        return Buf(name, t[:])

    def alias(self, name, parent, ap):
        b = Buf(name, ap)
        b.slots = parent.slots
        return b

    def pseudo(self, name):
        return Buf(name, None)

    def psum_init(self):
        for i in range(8):
            t = self.es.enter_context(self.nc.psum_tensor("psb%d" % i, [128, 512], F32))
            self.psum_banks.append(Buf("psb%d" % i, t[:]))

    def psum(self):
        b = self.psum_banks[self.psum_rr % 8]
        self.psum_rr += 1
        return b

    def _deps(self, eng, reads, writes):
        need = {}

        def add(tok):
            if tok is None:
                return
            k, v = tok
            if need.get(k, 0) < v:
                need[k] = v
        for b in reads:
            for s in b.slots:
                add(s.w)
        for b in writes:
            for s in b.slots:
                add(s.w)
                for k, v in s.r.items():
                    add((k, v))
        out = []
        for k, v in need.items():
            if eng.name == "pe" and k == eng.key:
                continue
            if eng.seen.get(k, 0) >= v:
                continue
            eng.seen[k] = v
            out.append((k, v))
        return out

    def _mark(self, tok, reads, writes):
        k, v = tok
        for b in writes:
            for s in b.slots:
                s.w = tok
                s.r = {}
        for b in reads:
            for s in b.slots:
                if s.r.get(k, 0) < v:
                    s.r[k] = v

    def op(self, engname, fn, reads=(), writes=(), signal=True):
        eng = self.engs[engname]
        deps = self._deps(eng, reads, writes)
        for k, v in deps:
            if k == eng.key:
                assert v <= eng.cnt, (engname, v, eng.cnt)
        if signal:
            eng.cnt += 1
            tok = (eng.key, eng.cnt)
        else:
            tok = (eng.key, eng.cnt + 1)
        self._mark(tok, reads, writes)
        self.nwaits += len(deps)
        self.ninst += 1
        eng.ops.append((deps, fn, signal))
        self.labels[engname].append(self.stage)

    def dma(self, qname, fn, buf, reads=(), writes=()):
        eng = self.engs[qname]
        deps = self._deps(eng, reads, writes)
        kind = "sw" if qname == "pool" else "hw"
        if not hasattr(buf, "dsems"):
            buf.dsems = {}
        if kind not in buf.dsems:
            sem = self.es.enter_context(self.nc.semaphore("d%s_%s" % (kind, buf.name)))
            buf.dsems[kind] = [sem, 0]
            self.sems["D%s_%s" % (kind, buf.name)] = sem
        buf.dsems[kind][1] += 16
        tok = ("D%s_%s" % (kind, buf.name), buf.dsems[kind][1])
        self._mark(tok, reads, writes)
        self.nwaits += len(deps)
        self.ninst += 1
        eng.ops.append((deps, fn, ("dma", buf.dsems[kind][0])))
        self.labels[qname].append(self.stage)

    def final_wait(self, qname, bufs):
        eng = self.engs[qname]
        deps = self._deps(eng, (), bufs)
        eng.ops.append((deps, None, False))

    def run(self):
        nc = self.nc
        K_ = self

        def replay(name):
            def body(e):
                eng = K_.engs[name]
                for deps, fn, signal in eng.ops:
                    for k, v in deps:
                        e.wait_ge(K_.sems[k], v)
                    if fn is None:
                        continue
                    inst = fn(e)
                    if signal is True:
                        inst.then_inc(eng.sem, 1)
                    elif signal is False:
                        pass
                    else:
                        inst.then_inc(signal[1], 16)
            return body
        with nc.Block() as block:
            block.tensor(replay("pe"))
            block.scalar(replay("act"))
            block.vector(replay("dve"))
            block.gpsimd(replay("pool"))
            block.sync(replay("sp"))


C_G1, C_G2, C_BF, C_L0, C_L1 = 0, 8, 16, 24, 32
C_HG, C_PS = 40, 44
C_W0, C_W1, C_W2, C_CB = 48, 70, 92, 114
C_PAD = 136
NCOLS = 144
HC_HBF, HC_A1, HC_NA1, HC_B1, HC_PADH, HC_HALF = 0, 8, 16, 24, 32, 33


def tiles_of_core():
    t = []
    b = 0
    while b < NBC:
        nb = min(4, NBC - b)
        t.append((b, nb))
        b += nb
    return t


def weight_schedule():
    s = []
    extra = {0: [("i0", 2048)], 1: [("i1", 2304)], 2: [("og0", 2560), ("og1", 2816)], 3: [("pl0", 3072), ("pl1", 3328)]}
    for hp in range(4):
        s.append(("q%d" % hp, "w_in", hp * 256, None, 8, 256))
        s.append(("f%d" % hp, "w_in", 1024 + hp * 256, None, 8, 256))
        if INTERLEAVE:
            for nm_, c_ in extra[hp]:
                s.append((nm_, "w_in", c_, None, 8, 256))
    if not INTERLEAVE:
        for hp in range(4):
            for nm_, c_ in extra[hp]:
                s.append((nm_, "w_in", c_, None, 8, 256))
    for hf in range(2):
        s.append(("wa%d" % hf, "w_a", hf * 512, None, 4, 512))
        s.append(("wb%d" % hf, "w_b", hf * 512, None, 4, 512))
        for i in range(2):
            up = hf * 2 + i
            s.append(("ga%d" % up, "w_in", 3584 + up * 256, None, 8, 256))
            s.append(("gb%d" % up, "w_in", 4608 + up * 256, None, 8, 256))
    for q in range(4):
        s.append(("wo%d" % q, "w_out", q * 256, None, 8, 256))
    for mp in range(11):
        s.append(("uu%d" % mp, "w_up", mp * 256, None, 8, 256))
        s.append(("uv%d" % mp, "w_up", 2816 + mp * 256, None, 8, 256))
    for q in range(4):
        for mg in range(3):
            nm = 8 if mg < 2 else 6
            s.append(("wd%d_%d" % (q, mg), "w_down", q * 256, mg * 8, nm, 256))
    return s


def build_program(debug=False, ntiles=None):
    nc = bass.Bass("TRN2", target_bir_lowering=False)
    dr = {}
    dumps = {}

    def din(name, shape, dt=F32):
        dr[name] = nc.dram_tensor(name, list(shape), dt, kind="ExternalInput").ap()
        return dr[name]

    xin = din("xin", [NBC * 128, D])
    din("w_in", [D, 5632])
    din("w_up", [D, 5632])
    din("w_down", [2816, D])
    din("w_out", [D, D])
    din("w_a", [512, D])
    din("w_b", [512, D])
    w_pool = din("w_pool", [512, 128])
    cols_d = din("cols", [128, NCOLS])
    gF_d = din("gF", [128, D])
    ident_d = din("ident", [128, 128], BF16)
    cmask_d = din("cmask", [128, 512], mybir.dt.uint32)
    smask_d = din("smask", [128, 512])
    onesbd_d = din("onesbd", [128, 128], BF16)
    poolm_d = din("poolm", [128, 12 * 128], BF16)
    out = nc.dram_tensor("out", [2 * 2048, D], F32, kind="ExternalOutput").ap()
    sched = weight_schedule()
    NW = len(sched)
    scr = nc.dram_tensor("wscr", [NW, 128, 2048], BF16, kind="Internal").ap()

    wsrc = {
        "w_in": dr["w_in"].rearrange("(kc p) c -> p kc c", p=128),
        "w_up": dr["w_up"].rearrange("(kc p) c -> p kc c", p=128),
        "w_out": dr["w_out"].rearrange("(kc p) c -> p kc c", p=128),
        "w_a": dr["w_a"].rearrange("(kc p) c -> p kc c", p=128),
        "w_b": dr["w_b"].rearrange("(kc p) c -> p kc c", p=128),
        "w_down": dr["w_down"].rearrange("(m p) c -> p m c", p=128),
    }

    with ExitStack() as es:
        k = K(nc, es)
        k.psum_init()
        cols = k.sb("cols", [128, NCOLS], F32)
        gF = k.sb("gF", [128, D], F32)
        ident = k.sb("ident", [128, 128], BF16)
        cmask = k.sb("cmask", [128, 512], mybir.dt.uint32)
        smask = k.sb("smask", [128, 512], F32)
        onesbd = k.sb("onesbd", [128, 128], BF16)
        poolm = k.sb("poolm", [128, 12, 128], BF16)
        wpool = k.sb("wpool", [128, 4, 128], BF16)
        lbt = k.sb("lbt", [128, 8], F32)
        omlt = k.sb("omlt", [128, 8], F32)
        nomlt = k.sb("nomlt", [128, 8], F32)
        ldiff = k.sb("ldiff", [128, 8], F32)
        for b_, d_ in ((cols, cols_d), (gF, gF_d), (ident, ident_d), (cmask, cmask_d), (smask, smask_d), (onesbd, onesbd_d)):
            k.dma("sp", lambda e, b_=b_, d_=d_: e.dma_start(out=b_[:], in_=d_), b_, writes=[b_])
        k.dma("sp", lambda e: e.dma_start(out=poolm[:], in_=poolm_d.rearrange("p (m t) -> p m t", t=128)), poolm, writes=[poolm])
        k.dma("pool", lambda e: e.dma_start(out=wpool[:], in_=w_pool.rearrange("(g c) d -> c g d", c=128)), wpool, writes=[wpool])
        k.op("dve", lambda e: e.tensor_tensor(out=ldiff[:], in0=cols[:, C_L0:C_L0 + 8], in1=cols[:, C_L1:C_L1 + 8], op=ALU.subtract), reads=[cols], writes=[ldiff])
        k.op("act", lambda e: e.activation(out=lbt[:], in_=ldiff[:], func=AF.Sigmoid), reads=[ldiff], writes=[lbt])
        k.op("act", lambda e: e.activation(out=omlt[:], in_=ldiff[:], func=AF.Sigmoid, scale=-1.0), reads=[ldiff], writes=[omlt])
        k.op("dve", lambda e: e.tensor_scalar(out=nomlt[:], in0=omlt[:], scalar1=-1.0, scalar2=None, op0=ALU.mult), reads=[omlt], writes=[nomlt])
        hcol = k.sb("hcol", [128, 40], F32)
        epsc = k.sb("epsc", [128, 1], F32)
        k.op("pool", lambda e: e.memset(epsc[:], EPS), writes=[epsc])
        k.op("dve", lambda e: e.tensor_scalar(out=hcol[:, HC_HBF:HC_HBF + 8], in0=cols[:, C_BF:C_BF + 8], scalar1=0.5, scalar2=None, op0=ALU.mult), reads=[cols], writes=[hcol])
        k.op("dve", lambda e: e.tensor_scalar(out=hcol[:, HC_A1:HC_A1 + 8], in0=omlt[:], scalar1=0.5, scalar2=None, op0=ALU.mult), reads=[omlt, hcol], writes=[hcol])
        k.op("dve", lambda e: e.tensor_scalar(out=hcol[:, HC_NA1:HC_NA1 + 8], in0=omlt[:], scalar1=-0.5, scalar2=None, op0=ALU.mult), reads=[omlt, hcol], writes=[hcol])
        k.op("dve", lambda e: e.tensor_scalar(out=hcol[:, HC_B1:HC_B1 + 8], in0=lbt[:], scalar1=0.5, scalar2=0.5, op0=ALU.mult, op1=ALU.add), reads=[lbt, hcol], writes=[hcol])
        k.op("dve", lambda e: e.tensor_scalar(out=hcol[:, HC_PADH:HC_PADH + 1], in0=cols[:, C_PAD:C_PAD + 1], scalar1=0.5, scalar2=None, op0=ALU.mult), reads=[cols, hcol], writes=[hcol])
        k.op("pool", lambda e: e.memset(hcol[:, HC_HALF:HC_HALF + 1], 0.5), reads=[hcol], writes=[hcol])

        S = k.sb("S", [128, 8, 64], F32)
        Sb = [[k.sb("Sb%d_%d" % (p_, i), [128, 8, 64], BF16) for i in range(2)] for p_ in range(2)]
        Stmp = k.sb("Stmp", [128, 8, 64], F32)
        halo = k.sb("halo", [128, 22, 2], F32)
        Mt = k.sb("Mt", [128, 8, 8], F32)
        dSt = k.sb("dSt", [128, 8, 8], F32)
        Et = k.sb("Et", [128, 8, 8], F32)
        ptok = [k.sb("ptok%d" % i, [128, 4, 512], BF16) for i in range(2)]
        k.op("pool", lambda e: e.memset(halo[:], 0.0), writes=[halo])

        hbuf = [k.sb("h%d" % i, [128, 4, D], F32) for i in range(2)]
        junk = k.sb("junk", [128, D], BF16)
        ss1, rs1, ss2, rs2, ssF, rsF = [k.sb(n_, [128, 4], F32) for n_ in ("ss1", "rs1", "ss2", "rs2", "ssF", "rsF")]
        ztok = k.sb("ztok", [128, 4, D], BF16)
        zf = [k.sb("zf%d" % i, [128, 512], BF16) for i in range(8)]
        qt = [k.sb("qt%d" % i, [128, 512], BF16) for i in range(8)]
        kt = [k.sb("kt%d" % i, [128, 512], BF16) for i in range(8)]
        kT = [k.sb("kT%d" % i, [128, 1024], BF16) for i in range(4)]
        z2f = [k.sb("z2f%d" % i, [128, 512], BF16) for i in range(8)]
        gg = qt + kt + [k.alias("gg%d" % (16 + i), kT[i // 2], kT[i // 2].ap[:, (i % 2) * 512:(i % 2 + 1) * 512]) for i in range(6)]
        vtok = k.sb("vtok", [128, 4, 512], BF16)
        sog = k.sb("sog", [128, 4, 512], BF16)
        oa = k.sb("oa", [128, 4, 512], BF16)
        ob = k.alias("ob", vtok, vtok.ap)
        dfeat = k.sb("dfeat", [128, 4, 512], BF16)
        mix = [k.alias("mix%d" % i, sog, sog.ap[:, i, :]) for i in range(4)] + [k.sb("mix%d" % i, [128, 512], BF16) for i in range(4, 8)]
        qs = [k.sb("qs%d" % i, [128, 512], F32) for i in range(2)]
        th = [k.sb("th%d" % i, [128, 512], F32) for i in range(2)]
        lf = [k.sb("lf%d" % i, [128, 512], F32) for i in range(2)]
        kk = [k.sb("kk%d" % i, [128, 512], F32) for i in range(2)]
        aa = k.sb("aa", [128, 512], F32)
        ar = k.sb("ar", [128, 512], F32)
        ep = k.sb("ep", [128, 512], F32)
        en = k.sb("en", [128, 512], F32)
        msk = [[k.sb("msk%d_%d" % (p_, i), [128, 512], BF16) for i in range(2)] for p_ in range(2)]
        for mm_ in msk:
            for m_ in mm_:
                k.op("pool", lambda e, m_=m_: e.memset(m_[:], 0.0), writes=[m_])
        osq = k.sb("osq", [128, 512], BF16)
        ors = k.sb("ors", [128, 512], F32)
        on = k.sb("on", [128, 512], F32)
        sga = k.sb("sga", [128, 512], BF16)
        sgb = k.sb("sgb", [128, 512], BF16)
        t1 = k.alias("t1", dfeat, dfeat.ap[:, 0:2, :].rearrange("p a b -> p (a b)").bitcast(F32))
        t2 = k.alias("t2", dfeat, dfeat.ap[:, 2:4, :].rearrange("p a b -> p (a b)").bitcast(F32))
        ub = [k.sb("ub%d" % i, [128, 516], F32) for i in range(2)]
        c0 = [k.sb("c0_%d" % i, [128, 512], F32) for i in range(2)]
        c2 = [k.sb("c2_%d" % i, [128, 512], F32) for i in range(2)]
        ot = k.sb("ot", [128, D], F32)
        c1 = [k.alias("c1_%d" % i, ot, ot.ap[:, i * 512:(i + 1) * 512]) for i in range(2)]
        ring = [k.sb("wr%d" % i, [128, 2048], BF16) for i in range(NS)]
        scrb = [k.pseudo("scr%d" % i) for i in range(NW)]

        tiles = tiles_of_core()
        if ntiles is not None:
            tiles = tiles[:ntiles]

        def dump(name, buf, ap=None):
            if not debug or name in dumps:
                return
            src = buf.ap if ap is None else ap
            dt_ = src.dtype
            d = nc.dram_tensor('dbg_' + name, list(src.shape), dt_, kind='ExternalOutput').ap()
            dumps[name] = d
            k.dma('sp', lambda e, d=d, src=src: e.dma_start(out=d, in_=src), buf, reads=[buf])
        glob = [(ti, wi) for ti in range(len(tiles)) for wi in range(NW)]
        wstate = dict(emitted=0, next=0)
        loaded = {}

        def emit_load(gi):
            ti, wi = glob[gi]
            key, which, a, b, nk, ncol = sched[wi]
            slot = ring[gi % NS]
            view = slot.ap[:, 0:nk * ncol].rearrange("p (k c) -> p k c", c=ncol)
            if ti == 0 or not USE_SCRATCH:
                if which == "w_down":
                    src = wsrc[which][:, b:b + nk, a:a + ncol]
                else:
                    src = wsrc[which][:, :, a:a + ncol]
                k.dma("pool", lambda e, view=view, src=src: e.dma_start(out=view, in_=src), slot, writes=[slot])
                if USE_SCRATCH:
                    k.dma("sp", lambda e, slot=slot, wi=wi: e.dma_start(out=scr[wi], in_=slot[:]), slot, reads=[slot], writes=[scrb[wi]])
            else:
                k.dma("sp", lambda e, slot=slot, wi=wi: e.dma_start(out=slot[:], in_=scr[wi]), slot, reads=[scrb[wi]], writes=[slot])
            loaded[gi] = (slot, view)

        released = set()
        cur_gi = {}

        def pump():
            while wstate["emitted"] < len(glob):
                g_ = wstate["emitted"]
                if g_ - NS >= 0 and (g_ - NS) not in released:
                    break
                emit_load(g_)
                wstate["emitted"] += 1

        def wget(key):
            gi = wstate["next"]
            wstate["next"] += 1
            assert sched[glob[gi][1]][0] == key, (sched[glob[gi][1]][0], key)
            pump()
            assert gi in loaded, ("weight ring too small", key)
            slot, view = loaded.pop(gi)
            cur_gi[slot.name] = gi
            return slot, view

        def wrel(*slots):
            for sl in slots:
                released.add(cur_gi.pop(sl.name))
            pump()

        def mm(out_, lhsT, rhs, start, stop):
            return lambda e: e.matmul(out_, lhsT=lhsT, rhs=rhs, start=start, stop=stop)

        def rstd_from_ss(ssb, rsb, n, inv):
            k.op("act", lambda e: e.activation(out=rsb[:, 0:n], in_=ssb[:, 0:n], func=AF.Ln, scale=inv, bias=epsc[:, 0:1]), reads=[ssb, epsc], writes=[rsb])
            k.op("act", lambda e: e.activation(out=rsb[:, 0:n], in_=rsb[:, 0:n], func=AF.Exp, scale=-0.5), reads=[rsb], writes=[rsb])

        def norm_part_a(hb, nb, ssb, rsb):
            for b in range(nb):
                k.op("act", lambda e, b=b: e.activation(out=junk[:], in_=hb[:, b, :], func=AF.Square, accum_out=ssb[:, b:b + 1]),
                     reads=[hb], writes=[junk, ssb])
            rstd_from_ss(ssb, rsb, nb, 1.0 / D)
            for b in range(nb):
                k.op("dve", lambda e, b=b: e.tensor_scalar(out=ztok[:, b, :], in0=hb[:, b, :], scalar1=rsb[:, b:b + 1], scalar2=None, op0=ALU.mult),
                     reads=[hb, rsb], writes=[ztok])

        def norm_part_b(nb, TT, gcol, zdst):
            for kc in range(8):
                pb = k.psum()
                pv = pb.ap.bitcast(BF16)
                for b in range(nb):
                    k.op("pe", lambda e, b=b, kc=kc, pv=pv: e.transpose(out=pv[:, b * 128:(b + 1) * 128], in_=ztok[:, b, kc * 128:(kc + 1) * 128], identity=ident[:]),
                         reads=[ztok, ident], writes=[pb], signal=(b == nb - 1))
                if kc % 2 == 0:
                    k.op("act", lambda e, kc=kc, pv=pv: e.activation(out=zdst[kc][:, 0:TT], in_=pv[:, 0:TT], func=AF.Identity, scale=cols[:, gcol + kc:gcol + kc + 1]),
                         reads=[pb, cols], writes=[zdst[kc]])
                else:
                    k.op("dve", lambda e, kc=kc, pv=pv: e.tensor_scalar(out=zdst[kc][:, 0:TT], in0=pv[:, 0:TT], scalar1=cols[:, gcol + kc:gcol + kc + 1], scalar2=None, op0=ALU.mult),
                         reads=[pb, cols], writes=[zdst[kc]])

        def feat_mm(pb, wslot, wview, col0, zsrc, TT):
            for kc in range(8):
                k.op("pe", mm(pb[:, 0:TT], wview[:, kc, col0:col0 + 128], zsrc[kc][:, 0:TT], kc == 0, kc == 7),
                     reads=[zsrc[kc], wslot], writes=[pb], signal=(kc == 7))

        def load_h(ti):
            gb0, nb = tiles[ti]
            hb = hbuf[ti % 2]
            k.stage = 't%d_s0' % ti
            k.dma("sp", lambda e: e.dma_start(out=hb[:, 0:nb, :], in_=xin.rearrange("(b p) d -> p b d", p=128)[:, gb0:gb0 + nb, :]), hb, writes=[hb])

        def norm1_a(ti):
            gb0, nb = tiles[ti]
            k.stage = 't%d_s1norm' % ti
            norm_part_a(hbuf[ti % 2], nb, ss1, rs1)

        def norm1_b(ti):
            gb0, nb = tiles[ti]
            k.stage = 't%d_s1norm' % ti
            norm_part_b(nb, nb * 128, C_G1, zf)

        def mixer(ti):
            gb0, nb = tiles[ti]
            h = hbuf[ti % 2]
            TT = nb * 128
            ncn = nb * 2
            pt_cur = ptok[ti % 2]
            pt_prev = ptok[(ti + 1) % 2]
            for i_ in range(8):
                dump('zf%d' % i_, zf[i_])
            def do_v(i):
                k.stage = 't%d_s2b_v' % ti
                slot, view = wget("i%d" % i)
                for b in range(nb):
                    pb = k.psum()
                    for kc in range(8):
                        k.op("pe", mm(pb[:, 0:256], zf[kc][:, b * 128:(b + 1) * 128], view[:, kc, :], kc == 0, kc == 7),
                             reads=[zf[kc], slot], writes=[pb], signal=(kc == 7))
                    k.op("dve", lambda e, pb=pb, b=b, i=i: e.tensor_copy(out=vtok[:, b, i * 256:(i + 1) * 256], in_=pb[:, 0:256]), reads=[pb], writes=[vtok])
                wrel(slot)

            def do_og(i):
                k.stage = 't%d_s2c_og' % ti
                slot, view = wget("og%d" % i)
                for jj in range(2):
                    u = i * 2 + jj
                    pb = k.psum()
                    feat_mm(pb, slot, view, jj * 128, zf, TT)
                    k.op("act", lambda e, pb=pb, u=u: e.activation(out=sog[:, u, 0:TT], in_=pb[:, 0:TT], func=AF.Silu), reads=[pb], writes=[sog])
                wrel(slot)

            def do_pl(i):
                k.stage = 't%d_s2d_p' % ti
                slot, view = wget("pl%d" % i)
                for b in range(nb):
                    pb = k.psum()
                    for kc in range(8):
                        k.op("pe", mm(pb[:, 0:256], zf[kc][:, b * 128:(b + 1) * 128], view[:, kc, :], kc == 0, kc == 7),
                             reads=[zf[kc], slot], writes=[pb], signal=(kc == 7))
                    k.op("pool", lambda e, pb=pb, b=b, i=i: None, reads=[], writes=[]) if False else None
                    k.op("dve", lambda e, pb=pb, b=b, i=i: e.tensor_copy(out=pt_cur[:, b, i * 256:(i + 1) * 256], in_=pb[:, 0:256]), reads=[pb], writes=[pt_cur])
                wrel(slot)

            extra = {0: [lambda: do_v(0)], 1: [lambda: do_v(1)], 2: [lambda: do_og(0), lambda: do_og(1)], 3: [lambda: do_pl(0), lambda: do_pl(1)]}
            for hp in range(4):
                k.stage = 't%d_s2a_qf' % ti
                qslot, qview = wget("q%d" % hp)
                fslot, fview = wget("f%d" % hp)
                pqs, pfs = [], []
                for jj in range(2):
                    pq = k.psum()
                    feat_mm(pq, qslot, qview, jj * 128, zf, TT)
                    pf = k.psum()
                    feat_mm(pf, fslot, fview, jj * 128, zf, TT)
                    pqs.append(pq)
                    pfs.append(pf)
                wrel(qslot, fslot)
                if INTERLEAVE:
                    for f_ in extra[hp]:
                        f_()
                k.stage = 't%d_s2a_qf' % ti
                for jj in range(2):
                    k.op("act", lambda e, pq=pqs[jj], q_=qs[jj]: e.activation(out=q_[:, 0:TT], in_=pq[:, 0:TT], func=AF.Silu), reads=[pqs[jj]], writes=[qs[jj]])
                for jj in range(2):
                    j = hp * 2 + jj
                    k.op("act", lambda e, pf=pfs[jj], t_=th[jj], j=j: e.activation(out=t_[:, 0:TT], in_=pf[:, 0:TT], func=AF.Tanh, scale=0.5, bias=hcol[:, HC_HBF + j:HC_HBF + j + 1]),
                         reads=[pfs[jj], hcol], writes=[th[jj]])
                for jj in range(2):
                    j = hp * 2 + jj
                    k.op("act", lambda e, t_=th[jj], l_=lf[jj], j=j: e.activation(out=l_[:, 0:TT], in_=t_[:, 0:TT], func=AF.Ln, scale=hcol[:, HC_A1 + j:HC_A1 + j + 1], bias=hcol[:, HC_B1 + j:HC_B1 + j + 1]),
                         reads=[th[jj], hcol], writes=[lf[jj]])
                    k.op("pool", lambda e, t_=th[jj], k_=kk[jj], j=j: e.tensor_scalar(out=k_[:, 0:TT], in0=t_[:, 0:TT], scalar1=hcol[:, HC_NA1 + j:HC_NA1 + j + 1], scalar2=hcol[:, HC_A1 + j:HC_A1 + j + 1], op0=ALU.mult, op1=ALU.add),
                         reads=[th[jj], hcol], writes=[kk[jj]])
                for jj in range(2):
                    j = hp * 2 + jj
                    k.op("dve", lambda e, l_=lf[jj]: e.tensor_tensor_scan(out=aa[:, 0:TT], data0=smask[:, 0:TT], data1=l_[:, 0:TT], initial=0.0, op0=ALU.mult, op1=ALU.add),
                         reads=[smask, lf[jj]], writes=[aa])
                    a3 = aa.ap.rearrange("p (c t) -> p c t", t=64)
                    ar3 = ar.ap.rearrange("p (c t) -> p c t", t=64)
                    ep3 = ep.ap.rearrange("p (c t) -> p c t", t=64)
                    k.op("dve", lambda e, a3=a3, ar3=ar3: e.tensor_tensor(out=ar3[:, 0:ncn, :], in0=a3[:, 0:ncn, :], in1=a3[:, 0:ncn, 31:32].to_broadcast([128, ncn, 64]), op=ALU.subtract),
                         reads=[aa], writes=[ar])
                    k.op("act", lambda e, a3=a3, j=j: e.activation(out=Mt[:, j, 0:ncn], in_=a3[:, 0:ncn, 31], func=AF.Exp), reads=[aa], writes=[Mt])
                    k.op("act", lambda e, a3=a3, j=j: e.activation(out=dSt[:, j, 0:ncn], in_=a3[:, 0:ncn, 63], func=AF.Exp), reads=[aa], writes=[dSt])
                    k.op("act", lambda e: e.activation(out=ep[:, 0:TT], in_=ar[:, 0:TT], func=AF.Exp), reads=[ar], writes=[ep])
                    k.op("act", lambda e: e.activation(out=en[:, 0:TT], in_=ar[:, 0:TT], func=AF.Exp, scale=-1.0), reads=[ar], writes=[en])
                    k.op("pool", lambda e, ep3=ep3, j=j: e.tensor_copy(out=Et[:, j, 0:ncn], in_=ep3[:, 0:ncn, 63]), reads=[ep], writes=[Et])
                    k.op("dve", lambda e, j=j, q_=qs[jj]: e.tensor_tensor(out=qt[j][:, 0:TT], in0=q_[:, 0:TT], in1=ep[:, 0:TT], op=ALU.mult), reads=[qs[jj], ep], writes=[qt[j]])
                    k.op("pool", lambda e, j=j, k_=kk[jj]: e.tensor_tensor(out=kt[j][:, 0:TT], in0=k_[:, 0:TT], in1=en[:, 0:TT], op=ALU.mult), reads=[kk[jj], en], writes=[kt[j]])
            if not INTERLEAVE:
                for hp_ in range(4):
                    for f_ in extra[hp_]:
                        f_()
            for i_ in range(8):
                dump('qt%d' % i_, qt[i_])
                dump('kt%d' % i_, kt[i_])
            dump('Mt', Mt)
            dump('dSt', dSt)
            dump('Et', Et)
            dump('vtok', vtok)
            dump('sog', sog)
            dump('ptok', pt_cur)
            k.stage = 't%d_s4_pool' % ti
            for b in range(nb):
                gb = gb0 + b
                pb = k.psum()
                for g in range(4):
                    cur = pt_cur[:, b, g * 128:(g + 1) * 128]
                    if gb % NBS == 0:
                        k.op("pe", mm(pb[:, g * 128:(g + 1) * 128], cur, poolm[:, g * 3 + 2, :], True, True), reads=[pt_cur, poolm], writes=[pb], signal=(g == 3))
                    else:
                        if b == 0:
                            prevb, prev = pt_prev, pt_prev[:, 3, g * 128:(g + 1) * 128]
                        else:
                            prevb, prev = pt_cur, pt_cur[:, b - 1, g * 128:(g + 1) * 128]
                        k.op("pe", mm(pb[:, g * 128:(g + 1) * 128], cur, poolm[:, g * 3 + 0, :], True, False), reads=[pt_cur, poolm], writes=[pb], signal=False)
                        k.op("pe", mm(pb[:, g * 128:(g + 1) * 128], prev, poolm[:, g * 3 + 1, :], False, True), reads=[prevb, poolm], writes=[pb], signal=(g == 3))
                k.op("dve", lambda e, pb=pb, b=b: e.tensor_copy(out=dfeat[:, :, b * 128:(b + 1) * 128], in_=pb.ap.rearrange("p (g t) -> p g t", t=128)),
                     reads=[pb], writes=[dfeat])
            k.stage = 't%d_s3_hgrn' % ti
            hg_state = {}

            def hg_a(b):
                gb = gb0 + b
                ca, cb = 2 * b, 2 * b + 1
                Sbb = Sb[b % 2]
                mskb = msk[b % 2]
                if gb % NBS == 0:
                    k.op("dve", lambda e: e.memset(S[:], 0.0), writes=[S])
                k.op("dve", lambda e, ca=ca, Sbb=Sbb: e.tensor_tensor(out=Sbb[0][:], in0=S[:], in1=Mt[:, :, ca:ca + 1].to_broadcast([128, 8, 64]), op=ALU.mult),
                     reads=[S, Mt], writes=[Sbb[0]])
                pb = k.psum()
                pv = pb.ap.bitcast(BF16)
                for j in range(8):
                    k.op("pe", lambda e, j=j, b=b, pv=pv: e.transpose(out=pv[:, j * 128:(j + 1) * 128], in_=kt[j][:, b * 128:(b + 1) * 128], identity=ident[:]),
                         reads=[kt[j], ident], writes=[pb], signal=(j == 7))
                k.op("act", lambda e, pv=pv, b=b: e.activation(out=kT[b][:], in_=pv, func=AF.Copy), reads=[pb], writes=[kT[b]])
                for half in range(2):
                    pb = k.psum()
                    for jj in range(4):
                        j = half * 4 + jj
                        k.op("pe", mm(pb[:, jj * 128:(jj + 1) * 128], kt[j][:, b * 128:(b + 1) * 128], qt[j][:, b * 128:(b + 1) * 128], True, True),
                             reads=[kt[j], qt[j]], writes=[pb], signal=(jj == 3))
                    k.op("dve", lambda e, pb=pb, half=half, mskb=mskb: e.copy_predicated(out=mskb[half][:], mask=cmask[:], data=pb[:]),
                         reads=[pb, cmask, mskb[half]], writes=[mskb[half]])
                P = []
                for cc in range(2):
                    pp = k.psum()
                    P.append(pp)
                    r0 = cc * 64
                    for j in range(8):
                        k.op("pe", mm(pp[:, j * 64:(j + 1) * 64], kT[b][r0:r0 + 64, j * 128:(j + 1) * 128], vtok[r0:r0 + 64, b, j * 64:(j + 1) * 64], True, True),
                             reads=[kT[b], vtok], writes=[pp], signal=(j == 7))
                for cc, cidx in ((0, ca), (1, cb)):
                    pp = P[cc]
                    k.op("dve", lambda e, cidx=cidx: e.tensor_tensor(out=S[:], in0=S[:], in1=dSt[:, :, cidx:cidx + 1].to_broadcast([128, 8, 64]), op=ALU.mult),
                         reads=[S, dSt], writes=[S])
                    k.op("dve", lambda e, cidx=cidx, pp=pp: e.tensor_tensor(out=Stmp[:], in0=pp.ap.rearrange("p (j v) -> p j v", v=64),
                                                                          in1=Et[:, :, cidx:cidx + 1].to_broadcast([128, 8, 64]), op=ALU.mult),
                         reads=[pp, Et], writes=[Stmp])
                    k.op("dve", lambda e: e.tensor_tensor(out=S[:], in0=S[:], in1=Stmp[:], op=ALU.add), reads=[S, Stmp], writes=[S])
                    if cc == 0:
                        k.op("dve", lambda e, cb=cb, Sbb=Sbb: e.tensor_tensor(out=Sbb[1][:], in0=S[:], in1=Mt[:, :, cb:cb + 1].to_broadcast([128, 8, 64]), op=ALU.mult),
                             reads=[S, Mt], writes=[Sbb[1]])

            def hg_b(b):
                Sbb = Sb[b % 2]
                mskb = msk[b % 2]
                po = k.psum()
                for j in range(8):
                    r0 = (j % 2) * 64
                    cbase = (j // 2) * 128
                    half, jj = j // 4, j % 4
                    k.op("pe", mm(po[r0:r0 + 64, cbase:cbase + 128], vtok[:, b, j * 64:(j + 1) * 64], mskb[half][:, jj * 128:(jj + 1) * 128], True, False),
                         reads=[vtok, mskb[half]], writes=[po], signal=False)
                    k.op("pe", mm(po[r0:r0 + 64, cbase:cbase + 64], Sbb[0][:, j, :], qt[j][:, b * 128:b * 128 + 64], False, False),
                         reads=[Sbb[0], qt[j]], writes=[po], signal=False)
                    k.op("pe", mm(po[r0:r0 + 64, cbase + 64:cbase + 128], Sbb[1][:, j, :], qt[j][:, b * 128 + 64:b * 128 + 128], False, True),
                         reads=[Sbb[1], qt[j]], writes=[po], signal=(j == 7))
                k.op("act", lambda e, po=po: e.activation(out=osq[:], in_=po[:], func=AF.Square), reads=[po], writes=[osq])
                pss = k.psum()
                k.op("pe", mm(pss[:], onesbd[:], osq[:], True, True), reads=[onesbd, osq], writes=[pss])
                k.op("act", lambda e, pss=pss: e.activation(out=ors[:], in_=pss[:], func=AF.Ln, scale=1.0 / 64, bias=epsc[:, 0:1]), reads=[pss, epsc], writes=[ors])
                k.op("act", lambda e: e.activation(out=ors[:], in_=ors[:], func=AF.Exp, scale=-0.5), reads=[ors], writes=[ors])
                k.op("dve", lambda e, po=po: e.tensor_tensor(out=on[:], in0=po[:], in1=ors[:], op=ALU.mult), reads=[po, ors], writes=[on])
                k.op("pool", lambda e, b=b: e.tensor_tensor(out=on.ap.rearrange("p (j t) -> p j t", t=128), in0=on.ap.rearrange("p (j t) -> p j t", t=128),
                                                          in1=sog[:, :, b * 128:(b + 1) * 128], op=ALU.mult), reads=[on, sog], writes=[on])
                k.op("pool", lambda e, b=b: e.tensor_tensor(out=oa[:, :, b * 128:(b + 1) * 128], in0=on.ap.rearrange("p (j t) -> p j t", t=128),
                                                          in1=cols[:, C_HG:C_HG + 4].unsqueeze(2).to_broadcast([128, 4, 128]), op=ALU.mult),
                     reads=[on, cols], writes=[oa])

            if HG_PIPE:
                hg_a(0)
                for b in range(nb):
                    if b + 1 < nb:
                        hg_a(b + 1)
                    hg_b(b)
            else:
                for b in range(nb):
                    hg_a(b)
                    hg_b(b)
            dump('oa', oa)
            dump('S', S)
            k.stage = 't%d_s4_pool' % ti
            for g in range(4):
                pb = k.psum()
                k.op("pe", mm(pb[:, 0:TT], wpool[:, g, :], dfeat[:, g, 0:TT], True, True), reads=[wpool, dfeat], writes=[pb])
                k.op("dve", lambda e, pb=pb, g=g: e.tensor_scalar(out=ob[:, g, 0:TT], in0=pb[:, 0:TT], scalar1=cols[:, C_PS + g:C_PS + g + 1], scalar2=None, op0=ALU.mult),
                     reads=[pb, cols], writes=[ob])
            dump('ob', ob)
            k.stage = 't%d_s5_gate' % ti
            for hf in range(2):
                waslot, waview = wget("wa%d" % hf)
                wbslot, wbview = wget("wb%d" % hf)
                for i in range(2):
                    up = hf * 2 + i
                    gaslot, gaview = wget("ga%d" % up)
                    gbslot, gbview = wget("gb%d" % up)
                    for jj in range(2):
                        u = up * 2 + jj
                        for (gslot, gview, wslot_, wview_, src, sgt, tt) in ((gaslot, gaview, waslot, waview, oa, sga, t1), (gbslot, gbview, wbslot, wbview, ob, sgb, t2)):
                            pg = k.psum()
                            feat_mm(pg, gslot, gview, jj * 128, zf, TT)
                            k.op("act", lambda e, pg=pg, sgt=sgt: e.activation(out=sgt[:, 0:TT], in_=pg[:, 0:TT], func=AF.Tanh, scale=0.5), reads=[pg], writes=[sgt])
                            pa = k.psum()
                            cu = (u % 4) * 128
                            for kc in range(4):
                                k.op("pe", mm(pa[:, 0:TT], wview_[:, kc, cu:cu + 128], src[:, kc, 0:TT], kc == 0, kc == 3),
                                     reads=[wslot_, src], writes=[pa], signal=(kc == 3))
                            k.op("dve", lambda e, pa=pa, sgt=sgt, tt=tt: e.scalar_tensor_tensor(out=tt[:, 0:TT], in0=sgt[:, 0:TT], scalar=1.0, in1=pa[:, 0:TT], op0=ALU.add, op1=ALU.mult),
                                 reads=[pa, sgt], writes=[tt])
                        k.op("pool", lambda e, u=u: e.tensor_tensor(out=mix[u][:, 0:TT], in0=t1[:, 0:TT], in1=t2[:, 0:TT], op=ALU.add), reads=[t1, t2], writes=[mix[u]])
                    wrel(gaslot, gbslot)
                wrel(waslot, wbslot)
            for i_ in range(8):
                dump('mix%d' % i_, mix[i_])
            k.stage = 't%d_s6_wout' % ti
            wos = [wget("wo%d" % q) for q in range(4)]
            for b in range(nb):
                gb = gb0 + b
                for q in range(4):
                    slot, view = wos[q]
                    pb = k.psum()
                    for u in range(8):
                        k.op("pe", mm(pb[:, 0:256], mix[u][:, b * 128:(b + 1) * 128], view[:, u, :], u == 0, u == 7),
                             reads=[mix[u], slot], writes=[pb], signal=(u == 7))
                    sc_ = hcol[:, HC_PADH:HC_PADH + 1] if gb % NBS == NBS - 1 else hcol[:, HC_HALF:HC_HALF + 1]
                    k.op("dve", lambda e, pb=pb, b=b, q=q, sc_=sc_: e.scalar_tensor_tensor(out=h[:, b, q * 256:(q + 1) * 256], in0=pb[:, 0:256], scalar=sc_,
                                                                                          in1=h[:, b, q * 256:(q + 1) * 256], op0=ALU.mult, op1=ALU.add),
                         reads=[pb, h, hcol], writes=[h])
                k.stage = 't%d_s7norm' % ti
                k.op("act", lambda e, b=b: e.activation(out=junk[:], in_=h[:, b, :], func=AF.Square, accum_out=ss2[:, b:b + 1]), reads=[h], writes=[junk, ss2])
                k.op("act", lambda e, b=b: e.activation(out=rs2[:, b:b + 1], in_=ss2[:, b:b + 1], func=AF.Ln, scale=1.0 / D, bias=epsc[:, 0:1]), reads=[ss2, epsc], writes=[rs2])
                k.op("act", lambda e, b=b: e.activation(out=rs2[:, b:b + 1], in_=rs2[:, b:b + 1], func=AF.Exp, scale=-0.5), reads=[rs2], writes=[rs2])
                k.op("dve", lambda e, b=b: e.tensor_scalar(out=ztok[:, b, :], in0=h[:, b, :], scalar1=rs2[:, b:b + 1], scalar2=None, op0=ALU.mult),
                     reads=[h, rs2], writes=[ztok])
                k.stage = 't%d_s6_wout' % ti
            wrel(*[w_[0] for w_ in wos])
            dump('h1', h)

        def ffn(ti, hooks):
            gb0, nb = tiles[ti]
            h = hbuf[ti % 2]
            TT = nb * 128
            k.stage = 't%d_s7norm' % ti
            norm_part_b(nb, TT, C_G2, z2f)
            k.stage = 't%d_s8_up' % ti
            pend = []
            for mp in range(11):
                if mp in hooks:
                    hooks[mp]()
                    k.stage = 't%d_s8_up' % ti
                uslot, uview = wget("uu%d" % mp)
                vslot, vview = wget("uv%d" % mp)
                for jj in range(2):
                    m = mp * 2 + jj
                    par = m % 2
                    pu = k.psum()
                    feat_mm(pu, uslot, uview, jj * 128, z2f, TT)
                    pvv = k.psum()
                    feat_mm(pvv, vslot, vview, jj * 128, z2f, TT)
                    ubm, c0m, c2m, c1m = ub[par], c0[par], c2[par], c1[par]
                    k.op("pool", lambda e, ubm=ubm, m=m: e.tensor_copy(out=ubm[:, 0:2], in_=halo[:, m, :]), reads=[halo], writes=[ubm])
                    k.op("act", lambda e, ubm=ubm, pu=pu: e.activation(out=ubm[:, 2:2 + TT], in_=pu[:, 0:TT], func=AF.Copy), reads=[pu], writes=[ubm])
                    k.op("pool", lambda e, ubm=ubm, m=m: e.tensor_copy(out=halo[:, m, :], in_=ubm[:, TT:TT + 2]), reads=[ubm], writes=[halo])
                    k.op("act", lambda e, ubm=ubm, c1m=c1m, m=m: e.activation(out=c1m[:, 0:TT], in_=ubm[:, 1:1 + TT], func=AF.Identity, scale=cols[:, C_W1 + m:C_W1 + m + 1],
                                                                             bias=cols[:, C_CB + m:C_CB + m + 1]),
                         reads=[ubm, cols], writes=[c1m])
                    k.op("pool", lambda e, ubm=ubm, c0m=c0m, m=m: e.tensor_scalar(out=c0m[:, 0:TT], in0=ubm[:, 0:TT], scalar1=cols[:, C_W0 + m:C_W0 + m + 1],
                                                                               scalar2=0.0, op0=ALU.mult, op1=ALU.add),
                         reads=[ubm, cols], writes=[c0m])
                    k.op("pool", lambda e, c0m=c0m, c1m=c1m: e.tensor_tensor(out=c0m[:, 0:TT], in0=c0m[:, 0:TT], in1=c1m[:, 0:TT], op=ALU.add),
                         reads=[c0m, c1m], writes=[c0m])
                    k.op("dve", lambda e, pu=pu, c0m=c0m, c2m=c2m, m=m: e.scalar_tensor_tensor(out=c2m[:, 0:TT], in0=pu[:, 0:TT], scalar=cols[:, C_W2 + m:C_W2 + m + 1],
                                                                                            in1=c0m[:, 0:TT], op0=ALU.mult, op1=ALU.add),
                         reads=[pu, c0m, cols], writes=[c2m])
                    def part_b(c2m=c2m, pvv=pvv, m=m):
                        k.op("act", lambda e: e.activation(out=c2m[:, 0:TT], in_=c2m[:, 0:TT], func=AF.Silu), reads=[c2m], writes=[c2m])
                        k.op("dve", lambda e: e.tensor_tensor(out=gg[m][:, 0:TT], in0=pvv[:, 0:TT], in1=c2m[:, 0:TT], op=ALU.mult),
                             reads=[pvv, c2m], writes=[gg[m]])
                    if pend:
                        pend.pop()()
                    pend.append(part_b)
                wrel(uslot, vslot)
            if pend:
                pend.pop()()
            for i_ in range(22):
                dump('gg%d' % i_, gg[i_])
            k.stage = 't%d_s9_down' % ti
            for q in range(4):
                if ('d', q) in hooks:
                    hooks[('d', q)]()
                    k.stage = 't%d_s9_down' % ti
                ws = [wget("wd%d_%d" % (q, mg)) for mg in range(3)]
                for b in range(nb):
                    pb = k.psum()
                    for m in range(22):
                        slot, view = ws[m // 8]
                        k.op("pe", mm(pb[:, 0:256], gg[m][:, b * 128:(b + 1) * 128], view[:, m % 8, :], m == 0, m == 21),
                             reads=[gg[m], slot], writes=[pb], signal=(m == 21))
                    k.op("dve", lambda e, pb=pb, b=b, q=q: e.tensor_tensor(out=h[:, b, q * 256:(q + 1) * 256], in0=pb[:, 0:256], in1=h[:, b, q * 256:(q + 1) * 256], op=ALU.add),
                         reads=[pb, h], writes=[h])
                wrel(*[w_[0] for w_ in ws])
            dump('h2', h)

        def final(ti):
            gb0, nb = tiles[ti]
            h = hbuf[ti % 2]
            k.stage = 't%d_s10_out' % ti
            for b in range(nb):
                k.op("act", lambda e, b=b: e.activation(out=junk[:], in_=h[:, b, :], func=AF.Square, accum_out=ssF[:, b:b + 1]), reads=[h], writes=[junk, ssF])
            rstd_from_ss(ssF, rsF, nb, 1.0 / D)
            for b in range(nb):
                gb = gb0 + b
                s_, j_ = gb // NBS, gb % NBS
                k.op("pool" if b % 2 else "dve", lambda e, b=b: e.tensor_scalar(out=ot[:], in0=h[:, b, :], scalar1=rsF[:, b:b + 1], scalar2=None, op0=ALU.mult),
                     reads=[h, rsF], writes=[ot]) if False else None
                k.op("dve", lambda e, b=b: e.scalar_tensor_tensor(out=ot[:], in0=h[:, b, :], scalar=rsF[:, b:b + 1], in1=gF[:], op0=ALU.mult, op1=ALU.mult),
                     reads=[h, rsF, gF], writes=[ot])
                if j_ == 0:
                    r0, r1, o0 = 16, 128, s_ * 2048
                elif j_ == NBS - 1:
                    r0, r1, o0 = 0, 16, s_ * 2048 + 2032
                else:
                    r0, r1, o0 = 0, 128, s_ * 2048 + j_ * 128 - 16
                k.dma("sp", lambda e, r0=r0, r1=r1, o0=o0: e.dma_start(out=out[o0:o0 + (r1 - r0), :], in_=ot[r0:r1, :]), ot, reads=[ot])

        NT = len(tiles)
        load_h(0)
        norm1_a(0)
        norm1_b(0)
        for ti in range(NT):
            mixer(ti)
            hooks = {}
            if ti + 1 < NT:
                load_h(ti + 1)
                hooks[2] = (lambda t=ti + 1: norm1_a(t))
                hooks[('d', 1)] = (lambda t=ti + 1: norm1_b(t))
            ffn(ti, hooks)
            final(ti)
        k.final_wait("sp", [ot])
        if debug:
            k.final_wait('sp', [Mt, dSt, Et, vtok, sog, ptok[0], oa, S, ob] + hbuf + zf + qt + kt + mix + gg)
        k.run()
        build_program.stats = (k.ninst, k.nwaits, {n: len(e.ops) for n, e in k.engs.items()})
        build_program.labels = k.labels
        build_program.sbuf_free = nc.sbuf_bytes_remaining
    return nc


def _consts():
    bf = ml_dtypes.bfloat16
    ident = np.eye(128, dtype=np.float32).astype(bf)
    s = np.arange(128)[:, None]
    t = np.arange(128)[None, :]
    cmask = np.ascontiguousarray(np.tile(((s // 64 == t // 64) & (s <= t)).astype(np.uint32), (1, 4)))
    smask = np.ones((128, 512), np.float32)
    smask[:, ::64] = 0.0
    onesbd = (s // 64 == t // 64).astype(np.float32).astype(bf)
    pm = np.zeros((12, 128, 128), np.float32)
    for g, w in enumerate((2, 4, 8, 16)):
        cur = ((s <= t) & (s > t - w)).astype(np.float32) / w - (s == t).astype(np.float32)
        prev = ((s - 128 <= t) & (s - 128 > t - w)).astype(np.float32) / w
        cnt = np.minimum(t + 1, w).astype(np.float32)
        first = ((s <= t) & (s > t - w)).astype(np.float32) / cnt - (s == t).astype(np.float32)
        pm[g * 3 + 0], pm[g * 3 + 1], pm[g * 3 + 2] = cur, prev, first
    poolm = np.ascontiguousarray(pm.transpose(1, 0, 2).reshape(128, 12 * 128)).astype(bf)
    return ident, cmask, smask, onesbd, poolm


_CACHE = {}


def prepare(x, meta_tokens, lb_logits, norm1_g, w_in, b_f, head_norm_g, w_pool, pool_scale,
           w_branch_a, w_branch_b, w_out, norm2_g, w_up, conv_w, conv_b, w_down, final_norm_g):
    f32 = np.float32
    x = np.asarray(x, f32)
    B = x.shape[0]
    ncore = 8
    meta = np.asarray(meta_tokens, f32)

    def colize(v, n):
        return np.asarray(v, f32).reshape(n, 128).T

    cols = np.zeros((128, NCOLS), f32)
    cols[:, C_G1:C_G1 + 8] = colize(norm1_g[0], 8)
    cols[:, C_G2:C_G2 + 8] = colize(norm2_g[0], 8)
    cols[:, C_BF:C_BF + 8] = colize(b_f[0], 8)
    cols[:, C_L0:C_L0 + 8] = colize(lb_logits[0], 8)
    cols[:, C_L1:C_L1 + 8] = colize(lb_logits[1], 8)
    cols[:, C_HG:C_HG + 4] = colize(head_norm_g[0], 4)
    cols[:, C_PS:C_PS + 4] = colize(pool_scale[0], 4)
    cw = np.asarray(conv_w[0], f32)
    cols[:, C_W0:C_W0 + 22] = colize(cw[0], 22)
    cols[:, C_W1:C_W1 + 22] = colize(cw[1], 22)
    cols[:, C_W2:C_W2 + 22] = colize(cw[2], 22)
    cols[:, C_CB:C_CB + 22] = colize(conv_b[0], 22)
    cols[:16, C_PAD] = 1.0
    gF = np.ascontiguousarray(np.broadcast_to(np.asarray(final_norm_g, f32)[None, :], (128, D)))
    ident, cmask, smask, onesbd, poolm = _consts()
    shared = dict(
        w_in=np.ascontiguousarray(np.asarray(w_in, f32)[0]), w_up=np.ascontiguousarray(np.asarray(w_up, f32)[0]),
        w_down=np.ascontiguousarray(np.asarray(w_down, f32)[0]), w_out=np.ascontiguousarray(np.asarray(w_out, f32)[0]),
        w_a=np.ascontiguousarray(np.asarray(w_branch_a, f32)[0]), w_b=np.ascontiguousarray(np.asarray(w_branch_b, f32)[0]),
        w_pool=np.ascontiguousarray(np.asarray(w_pool, f32)[0].reshape(512, 128)),
        cols=cols, gF=gF, ident=ident, cmask=cmask, smask=smask, onesbd=onesbd, poolm=poolm)
    in_maps = []
    for c in range(ncore):
        xin = np.zeros((2, SEQP, D), f32)
        for s in range(2):
            xin[s, 0:16] = meta
            xin[s, 16:16 + 2048] = x[c * 2 + s]
        m = dict(shared)
        m["xin"] = xin.reshape(2 * SEQP, D)
        in_maps.append(m)
    return in_maps


def kernel(**inputs):
    f32 = np.float32
    ncore = 8
    in_maps = prepare(**inputs)
    if "nc" not in _CACHE:
        _CACHE["nc"] = build_program()
    nc = _CACHE["nc"]
    res = run_bass_kernel_spmd(nc, in_maps, core_ids=list(range(ncore)))
    outs = [np.asarray(r["out"], f32).reshape(2, 2048, D) for r in res.results]
    return np.concatenate(outs, axis=0)
```
